# Optimizing a Trainium2 kernel written in Bass

```python
import math
import jax, jax.numpy as jnp
from jax import lax
import numpy as np

D_MODEL = 1024
BATCH = 16
SEQ = 256
DEPTH = 1
DEC_BATCH = 8
DEC_SEQ = 1024
PAST_LEN = 512

GRID_W = 64
N_DIFF_HEADS = 6
DIFF_HEAD_DIM = 64
DIFF_V_DIM = 2 * DIFF_HEAD_DIM
N_FOURIER_HEADS = 4
FOURIER_HEAD_DIM = 64
D_FOURIER = N_FOURIER_HEADS * FOURIER_HEAD_DIM
D_ATTN = N_DIFF_HEADS * DIFF_V_DIM
D_MIX = D_FOURIER + D_ATTN
D_QK = N_DIFF_HEADS * 2 * DIFF_HEAD_DIM
D_IN = D_FOURIER + 2 * D_QK + D_ATTN
D_FF = 2816
CONV_W = 3
ROPE_THETA = 10000.0
EPS = 1e-6
Q_BLOCK = 128

kernel_name = 'hybrid_fnet_diffattn_dit_step'


def rmsnorm(x, g):
    xf = x.astype(jnp.float32)
    y = xf * lax.rsqrt(jnp.mean(xf * xf, axis=-1, keepdims=True) + EPS)
    return (y * g.astype(jnp.float32)).astype(x.dtype)


def ada_modulation(cond, w_ada, b_ada):
    m = jax.nn.silu(cond) @ w_ada + b_ada
    return jnp.split(m[:, None, :], 6, axis=-1)


def axial_rope_tables(n_tokens):
    rows = n_tokens // GRID_W
    row = jnp.repeat(jnp.arange(rows, dtype=jnp.float32), GRID_W)
    col = jnp.tile(jnp.arange(GRID_W, dtype=jnp.float32), rows)
    n_freq = DIFF_HEAD_DIM // 4
    inv = ROPE_THETA ** (-jnp.arange(n_freq, dtype=jnp.float32) / n_freq)
    ang_r = (row[:, None] * inv)[:, None, :]
    ang_c = (col[:, None] * inv)[:, None, :]
    return (jnp.cos(ang_r), jnp.sin(ang_r), jnp.cos(ang_c), jnp.sin(ang_c))


def rope_half(x, cos, sin):
    x1, x2 = jnp.split(x, 2, axis=-1)
    cos = cos.astype(x.dtype)
    sin = sin.astype(x.dtype)
    return jnp.concatenate([x1 * cos - x2 * sin, x2 * cos + x1 * sin], axis=-1)


def apply_axial_rope(x, cos_r, sin_r, cos_c, sin_c):
    xr, xc = jnp.split(x, 2, axis=-1)
    return jnp.concatenate([rope_half(xr, cos_r, sin_r), rope_half(xc, cos_c, sin_c)], axis=-1)


def fourier_mix(xf):
    B, S, _ = xf.shape
    z = xf.reshape(B, S, N_FOURIER_HEADS, FOURIER_HEAD_DIM).astype(jnp.float32)
    y = jnp.fft.fftn(z, axes=(1, 3), norm='ortho').real
    return y.reshape(B, S, D_FOURIER).astype(xf.dtype)


def diff_attention(q, k, v, lam):
    B, H, Sq = q.shape[:3]
    nb = Sq // Q_BLOCK
    qb = q.reshape(B, H, nb, Q_BLOCK, 2, DIFF_HEAD_DIM).transpose(2, 0, 1, 3, 4, 5)
    scale = DIFF_HEAD_DIM ** -0.5

    def block(qblk):
        s = jnp.einsum('bhqcd,bhkcd->bhcqk', qblk, k).astype(jnp.float32) * scale
        p = jax.nn.softmax(s, axis=-1)
        a = p[:, :, 0] - lam * p[:, :, 1]
        return jnp.einsum('bhqk,bhkd->bhqd', a.astype(v.dtype), v)

    o = lax.map(block, qb)
    return o.transpose(1, 2, 0, 3, 4).reshape(B, H, Sq, DIFF_V_DIM)


def conv_glu(h, w_gate, w_up, conv_w, conv_b, w_down):
    g = h @ w_gate
    gp = jnp.pad(g, ((0, 0), (1, 1), (0, 0)))
    g = gp[:, :-2] * conv_w[0] + gp[:, 1:-1] * conv_w[1] + gp[:, 2:] * conv_w[2] + conv_b
    return (jax.nn.silu(g) * (h @ w_up)) @ w_down


def trunk_layer(x, cond, lp, layer_idx, rope, k_ctx, v_ctx):
    (n1, n2, w_ada, b_ada, w_in, qg, kg, lq1, lk1, lq2, lk2, subg, w_out,
     w_gate, w_up, conv_w, conv_b, w_down) = lp
    B, S, _ = x.shape
    shift1, scale1, gate1, shift2, scale2, gate2 = ada_modulation(cond, w_ada, b_ada)

    h = rmsnorm(x, n1) * (1 + scale1) + shift1
    proj = h @ w_in
    xf, q, k, v = jnp.split(proj, [D_FOURIER, D_FOURIER + D_QK, D_FOURIER + 2 * D_QK], axis=-1)
    q = rmsnorm(q.reshape(B, S, N_DIFF_HEADS, 2, DIFF_HEAD_DIM), qg).transpose(0, 2, 1, 3, 4)
    k = rmsnorm(k.reshape(B, S, N_DIFF_HEADS, 2, DIFF_HEAD_DIM), kg).transpose(0, 2, 1, 3, 4)
    v = v.reshape(B, S, N_DIFF_HEADS, DIFF_V_DIM).transpose(0, 2, 1, 3)
    if rope is not None:
        q = apply_axial_rope(q, *rope)
        k = apply_axial_rope(k, *rope)
    if k_ctx is not None:
        k_all = jnp.concatenate([k, k_ctx.astype(k.dtype)], axis=2)
        v_all = jnp.concatenate([v, v_ctx.astype(v.dtype)], axis=2)
    else:
        k_all, v_all = k, v
    lam_init = 0.8 - 0.6 * math.exp(-0.3 * layer_idx)
    lam = (jnp.exp(jnp.sum(lq1.astype(jnp.float32) * lk1.astype(jnp.float32)))
           - jnp.exp(jnp.sum(lq2.astype(jnp.float32) * lk2.astype(jnp.float32))) + lam_init)
    o = diff_attention(q, k_all, v_all, lam)
    o = rmsnorm(o, subg) * (1.0 - lam_init)
    o = o.transpose(0, 2, 1, 3).reshape(B, S, D_ATTN)
    mix = jnp.concatenate([fourier_mix(xf), o], axis=-1) @ w_out
    x = x + gate1 * mix

    h2 = rmsnorm(x, n2) * (1 + scale2) + shift2
    x = x + gate2 * conv_glu(h2, w_gate, w_up, conv_w, conv_b, w_down)
    return x, k, v


def setup_inputs(seed: int = 0) -> dict:
    key = jax.random.key(seed)
    ks = jax.random.split(key, 26)
    f32 = jnp.float32

    def nrm(k, shape, s):
        return jax.random.normal(k, shape, f32) * s

    return {
        'x_prompt': nrm(ks[0], (BATCH, SEQ, D_MODEL), 1.0),
        'x_sample': nrm(ks[1], (DEC_BATCH, DEC_SEQ, D_MODEL), 1.0),
        'cache_k': nrm(ks[2], (DEC_BATCH, DEPTH, N_DIFF_HEADS, PAST_LEN, 2, DIFF_HEAD_DIM), 1.0),
        'cache_v': nrm(ks[3], (DEC_BATCH, DEPTH, N_DIFF_HEADS, PAST_LEN, DIFF_V_DIM), 1.0),
        'c': nrm(ks[4], (DEC_BATCH, D_MODEL), 1.0),
        'c_ctx': nrm(ks[5], (D_MODEL,), 1.0),
        'norm1_g': 1.0 + nrm(ks[6], (DEPTH, D_MODEL), 0.02),
        'norm2_g': 1.0 + nrm(ks[7], (DEPTH, D_MODEL), 0.02),
        'w_ada': nrm(ks[8], (DEPTH, D_MODEL, 6 * D_MODEL), 0.5 * D_MODEL ** -0.5),
        'b_ada': nrm(ks[9], (DEPTH, 6 * D_MODEL), 0.02),
        'w_in': nrm(ks[10], (DEPTH, D_MODEL, D_IN), D_MODEL ** -0.5),
        'q_norm_g': 1.0 + nrm(ks[11], (DEPTH, DIFF_HEAD_DIM), 0.02),
        'k_norm_g': 1.0 + nrm(ks[12], (DEPTH, DIFF_HEAD_DIM), 0.02),
        'lam_q1': nrm(ks[13], (DEPTH, DIFF_HEAD_DIM), 0.1),
        'lam_k1': nrm(ks[14], (DEPTH, DIFF_HEAD_DIM), 0.1),
        'lam_q2': nrm(ks[15], (DEPTH, DIFF_HEAD_DIM), 0.1),
        'lam_k2': nrm(ks[16], (DEPTH, DIFF_HEAD_DIM), 0.1),
        'subln_g': 1.0 + nrm(ks[17], (DEPTH, DIFF_V_DIM), 0.02),
        'w_out': nrm(ks[18], (DEPTH, D_MIX, D_MODEL), D_MIX ** -0.5),
        'w_gate': nrm(ks[19], (DEPTH, D_MODEL, D_FF), D_MODEL ** -0.5),
        'w_up': nrm(ks[20], (DEPTH, D_MODEL, D_FF), D_MODEL ** -0.5),
        'conv_w': nrm(ks[21], (DEPTH, CONV_W, D_FF), CONV_W ** -0.5),
        'conv_b': nrm(ks[22], (DEPTH, D_FF), 0.02),
        'w_down': nrm(ks[23], (DEPTH, D_FF, D_MODEL), D_FF ** -0.5),
    }


def reference(x_prompt, x_sample, cache_k, cache_v, c, c_ctx,
              norm1_g, norm2_g, w_ada, b_ada, w_in, q_norm_g, k_norm_g,
              lam_q1, lam_k1, lam_q2, lam_k2, subln_g, w_out,
              w_gate, w_up, conv_w, conv_b, w_down):
    rope = axial_rope_tables(x_sample.shape[1])
    y_p = x_prompt
    y_s = x_sample
    k_list, v_list = [], []
    for i in range(DEPTH):
        lp = (norm1_g[i], norm2_g[i], w_ada[i], b_ada[i], w_in[i], q_norm_g[i], k_norm_g[i],
              lam_q1[i], lam_k1[i], lam_q2[i], lam_k2[i], subln_g[i], w_out[i],
              w_gate[i], w_up[i], conv_w[i], conv_b[i], w_down[i])
        y_p, k_i, v_i = trunk_layer(y_p, c_ctx[None, :], lp, i, None, None, None)
        k_list.append(k_i)
        v_list.append(v_i)
        y_s, _, _ = trunk_layer(y_s, c, lp, i, rope, cache_k[:, i], cache_v[:, i])
    new_k = jnp.stack(k_list, axis=1)
    new_v = jnp.stack(v_list, axis=1)
    return (y_p, y_s, new_k, new_v)
```

```python
import os
import numpy as np
import ml_dtypes
import concourse.bass as bass
import concourse.mybir as mybir
from concourse.bass_utils import run_bass_kernel_spmd
from contextlib import ExitStack

F32 = mybir.dt.float32
BF16 = mybir.dt.bfloat16
AF = mybir.ActivationFunctionType
ALU = mybir.AluOpType

D = 1024
H = 6
DFF = 2816
NFC = DFF // 128
EPS = 1e-6
LAM_INIT = 0.8 - 0.6 * 1.0
NCORES = 8
NTOK = 1536
NV = 172
V_BADA, V_N1, V_N2, V_CS, V_CC, V_CW, V_CB, V_QG, V_KG, V_SG, V_SIGN = 0, 48, 56, 64, 72, 80, 146, 168, 169, 170, 171

DEBUG = os.environ.get("KDEBUG", "")


class Sched:
    def __init__(self, nc, es):
        self.nc = nc
        self.es = es
        self.eng = {"pe": nc.tensor, "act": nc.scalar, "dve": nc.vector, "pool": nc.gpsimd, "sp": nc.sync}
        self.sem = {e: es.enter_context(nc.semaphore("s_" + e)) for e in self.eng}
        self.cnt = {e: 0 for e in self.eng}
        self.waited = {e: {} for e in self.eng}
        self.last_w = {}
        self.readers = {}
        self.dsem = {}
        self.out_tokens = []

    def _deps(self, eng, reads, writes):
        deps = []
        for k in reads:
            t = self.last_w.get(k)
            if t is not None:
                if not (t[0] == "E" and t[1] == eng and eng == "pe"):
                    deps.append(t)
        for k in writes:
            t = self.last_w.get(k)
            if t is not None and not (t[0] == "E" and t[1] == eng and eng == "pe"):
                deps.append(t)
            for r in self.readers.get(k, ()):
                if not (r[0] == "E" and r[1] == eng and eng == "pe"):
                    deps.append(r)
        return deps

    def _emit_waits(self, eng, deps):
        h = self.eng[eng]
        w = self.waited[eng]
        need = {}
        for t in deps:
            key = (t[0], t[1])
            if t[2] > w.get(key, 0) and t[2] > need.get(key, 0):
                need[key] = t[2]
        for key, val in need.items():
            if key[0] == "E":
                h.wait_ge(self.sem[key[1]], val)
            else:
                h.wait_ge(self.dsem[key[1]][0], val)
            w[key] = val

    def _commit(self, tok, reads, writes):
        for k in writes:
            self.last_w[k] = tok
            self.readers[k] = []
        for k in reads:
            self.readers.setdefault(k, []).append(tok)

    def op(self, eng, fn, reads=(), writes=()):
        deps = self._deps(eng, reads, writes)
        self._emit_waits(eng, deps)
        ins = fn(self.eng[eng])
        ins.then_inc(self.sem[eng], 1)
        self.cnt[eng] += 1
        tok = ("E", eng, self.cnt[eng])
        self._commit(tok, reads, writes)
        return tok

    def dma(self, queue, fns, sem, reads=(), writes=(), is_output=False):
        if sem not in self.dsem:
            self.dsem[sem] = [self.es.enter_context(self.nc.semaphore("d_" + sem)), 0]
        deps = self._deps("#dma", reads, writes)
        self._emit_waits(queue, deps)
        h = self.eng[queue]
        ent = self.dsem[sem]
        for fn in fns:
            fn(h).then_inc(ent[0], 16)
            ent[1] += 16
        tok = ("D", sem, ent[1])
        self._commit(tok, reads, writes)
        if is_output:
            self.out_tokens.append(tok)
        return tok

    def retire(self, keys):
        toks = []
        for k in keys:
            t = self.last_w.pop(k, None)
            if t is not None:
                toks.append(t)
            toks.extend(self.readers.pop(k, []))
        return toks

    def seed(self, key, toks):
        self.readers.setdefault(key, []).extend(toks)

    def all_tokens(self):
        toks = [("E", e, c) for e, c in self.cnt.items() if c > 0]
        toks += [("D", n, v[1]) for n, v in self.dsem.items() if v[1] > 0]
        return toks

    def finish(self):
        self._emit_waits("sp", self.out_tokens)


def pipeline(units, order=None):
    if not units:
        return
    ns = max(len(u) for u in units)
    for t in range(len(units) + ns - 1):
        for j in range(ns):
            u = t - j
            if 0 <= u < len(units) and j < len(units[u]):
                units[u][j]()


def build_nc(debug=""):
    nc = bass.Bass("TRN2", target_bir_lowering=False)

    def din(name, shape, dt=F32):
        return nc.dram_tensor(name, list(shape), dt, kind="ExternalInput").ap()

    def dout(name, shape, dt=F32):
        return nc.dram_tensor(name, list(shape), dt, kind="ExternalOutput").ap()

    xin = din("xin", [NTOK, D])
    ck = din("ck", [H, 512, 128])
    cv = din("cv", [H, 512, 128])
    vecT = din("vecT", [128, NV])
    lamv = din("lamv", [1, 256])
    qkg = din("qkg", [1, 128])
    b_ada = din("b_ada", [1, 6 * D])
    w_ada = din("w_ada", [D, 6 * D])
    w_in = din("w_in", [D, 2560])
    w_out = din("w_out", [D, D])
    w_gate = din("w_gate", [D, DFF])
    w_up = din("w_up", [D, DFF])
    w_down = din("w_down", [DFF, D])
    c_ident = din("c_ident", [128, 128])
    c_bf = din("c_bf", [128, 4, 128], BF16)
    c_rope = din("c_rope", [128, 2, 1024])
    c_dftc = din("c_dftc", [128, 256], BF16)
    c_dfts = din("c_dfts", [128, 8, 1024], BF16)
    c_dftp = din("c_dftp", [128, 2, 512], BF16)

    yout = dout("yout", [NTOK, D])
    nk = dout("nk", [2, H, 256, 128])
    nv = dout("nv", [2, H, 256, 128])
    dbg = {}

    es = ExitStack()
    with es:
        S = Sched(nc, es)

        def sb(name, shape, dt=F32):
            return es.enter_context(nc.sbuf_tensor(name, list(shape), dt))

        ps_all = es.enter_context(nc.psum_tensor("ps_all", [128, 4096], F32))

        def bank(i, n=512, off=0):
            return ps_all[:, i * 512 + off: i * 512 + off + n]

        def PB(i):
            return ("ps", i)

        vec = sb("vec", [128, NV])
        ident = sb("ident", [128, 128])
        cbf = sb("cbf", [128, 4, 128], BF16)
        blk64, ones_bf, mean128, perm = cbf[:, 0, :], cbf[:, 1, :], cbf[:, 2, :], cbf[:, 3, :]
        mods = sb("mods", [128, 6, 8, 2])
        G1 = sb("G1", [128, 8, 2])
        G2 = sb("G2", [128, 8, 2])
        sT = sb("sT", [128, 8, 2], BF16)
        srep = sb("srep", [128, 2, 8, 128], BF16)
        lam_t = sb("lam_t", [128, 8])
        lamb = sb("lamb", [128, 256])
        qkg_bc = sb("qkg_bc", [128, 128])
        epsc = sb("epsc", [128, 1])
        tmp16 = sb("tmp16", [128, 3, 16])
        dftc = sb("dftc", [128, 256], BF16)[:]
        dftp = sb("dftp", [128, 2, 512], BF16)[:]
        lamjunk = sb("lamjunk", [128, 128])
        gate_t = [sb("gate%d" % i, [128, D]) for i in range(2)]
        bada_bc = sb("bada_bc", [128, D])
        junk = sb("junk", [128, D], BF16)
        stat = sb("stat", [128, 12, 4])

        S.dma("sp", [lambda h: h.dma_start(out=vec[:], in_=vecT)], "vec", writes=["vec"])
        S.dma("sp", [lambda h: h.dma_start(out=ident[:], in_=c_ident)], "ident", writes=["ident"])
        S.dma("sp", [lambda h: h.dma_start(out=cbf[:], in_=c_bf)], "cbf", writes=["cbf"])
        S.dma("sp", [lambda h: h.dma_start(out=lamb[:], in_=lamv.partition_broadcast(128))], "lamb", writes=["lamb"])
        S.dma("sp", [lambda h: h.dma_start(out=qkg_bc[:], in_=qkg.partition_broadcast(128))], "qkg", writes=["qkg"])
        S.op("dve", lambda h: h.memset(epsc[:], EPS), writes=["epsc"])

        A0 = 0
        A1 = 24576
        A2 = 40960
        ARENA_BYTES = A2 + 144384
        arena = sb("arena", [128, ARENA_BYTES // 4])

        def carve(off, shape, dt=F32):
            n = 1
            for s_ in shape[1:]:
                n *= s_
            assert off % 4 == 0
            if dt == F32:
                assert off + 4 * n <= ARENA_BYTES, (off, shape)
                ap = arena[:, off // 4: off // 4 + n]
            else:
                assert n % 2 == 0 and off + 2 * n <= ARENA_BYTES, (off, shape)
                ap = arena[:, off // 4: off // 4 + n // 2].bitcast(BF16)
            if len(shape) == 3:
                ap = ap.rearrange("p (a b) -> p a b", a=shape[1])
            elif len(shape) == 4:
                ap = ap.rearrange("p (a b c) -> p a b c", a=shape[1], b=shape[2])
            return ap

        hT = carve(A0, [128, 8, NTOK], BF16)
        mixT = hT
        wa_slots = [carve(A1 + i * 8192, [128, 8, 512], BF16) for i in range(2)]
        o = A2
        V_own = carve(o, [128, 12, 768], BF16); o += 18432
        V_cache = carve(o, [128, 4, 768], BF16); o += 6144
        kT_cache = carve(o, [128, 6, 512], BF16); o += 6144
        o += 2048
        xfT = carve(o, [128, 2, NTOK], BF16); o += 6144
        qT = carve(o, [128, 6, NTOK], BF16); o += 18432
        kT_own = carve(o, [128, 6, NTOK], BF16); o += 18432
        NXT = 12
        xt = [carve(A2 + i * 4096, [128, D]) for i in range(8)] + [carve(o - 16384 + i * 4096, [128, D]) for i in range(4)]
        P23_END = o
        win_slots = [carve(o + i * 8192, [128, 8, 512], BF16) for i in range(3)]; o += 24576
        sq_r = [carve(o + i * 1024, [128, 512], BF16) for i in range(2)]; o += 2048
        qnb_r = [carve(o + i * 1024, [128, 512], BF16) for i in range(2)]; o += 2048
        ln_r = [carve(o + i * 2048, [128, 512]) for i in range(2)]; o += 4096
        qn_r = [carve(o + i * 2048, [128, 512]) for i in range(2)]; o += 4096
        t1_r = [carve(o + i * 2048, [128, 512]) for i in range(2)]; o += 4096
        ropeT = carve(o, [128, 2, 1024]); o += 8192
        kst_r = [carve(o + i * 3072, [128, 768]) for i in range(2)]; o += 6144
        vst_r = [carve(o + i * 3072, [128, 768]) for i in range(2)]; o += 6144
        ksq = carve(o, [128, 768]); o += 3072
        ckst_r = [carve(o + i * 2048, [128, 4, 128]) for i in range(2)]; o += 4096
        assert o <= ARENA_BYTES, o

        wada_v = w_ada.rearrange("(k p) n -> p k n", p=128)
        win_v = w_in.rearrange("(k p) n -> p k n", p=128)
        wa_cnt = [0]
        kst_off = A2 + 75776 + 24576 + 2048 + 2048 + 4096 + 4096 + 4096 + 8192
        wa_slots = wa_slots + [carve(kst_off + i * 8192, [128, 8, 512], BF16) for i in range(2)]

        def load_wada_half(j, hf):
            slot = wa_cnt[0] % 2 if wa_cnt[0] >= 4 else wa_cnt[0]
            wa_cnt[0] += 1
            key = ("wa", slot)
            c0 = j * 1024 + hf * 512
            S.dma("pool", [lambda h, k=k: h.dma_start(out=wa_slots[slot][:, k, :], in_=wada_v[:, k, c0:c0 + 512])
                           for k in range(8)], "wa%d" % slot, writes=[key])
            return slot

        cond = vec[:, V_CS:V_CS + 16]
        S.op("act", lambda h: h.activation(out=tmp16[:, 0, :], in_=cond, func=AF.Exp, scale=-1.0),
             reads=["vec"], writes=["t16a"])
        S.op("dve", lambda h: h.tensor_scalar(out=tmp16[:, 1, :], in0=tmp16[:, 0, :], scalar1=1.0, scalar2=None, op0=ALU.add),
             reads=["t16a"], writes=["t16b"])
        S.op("dve", lambda h: h.reciprocal(out=tmp16[:, 2, :], in_=tmp16[:, 1, :]), reads=["t16b"], writes=["t16c"])
        S.op("dve", lambda h: h.tensor_tensor(out=sT[:].rearrange("p k c -> p c k"),
                                              in0=cond.rearrange("p (c k) -> p c k", c=2),
                                              in1=tmp16[:, 2, :].rearrange("p (c k) -> p c k", c=2), op=ALU.mult),
             reads=["t16c", "vec"], writes=["sT"])
        for c in range(2):
            S.op("dve", lambda h, c=c: h.tensor_copy(out=srep[:, c, :, :], in_=sT[:, :, c:c + 1].to_broadcast([128, 8, 128])),
                 reads=["sT"], writes=[("srep", c)])
        S.op("dve", lambda h: h.tensor_tensor(out=lamjunk[:].rearrange("p (a b) -> p a b", a=2),
                                              in0=lamb[:].rearrange("p (a c b) -> p a c b", a=2, c=2)[:, :, 0, :],
                                              in1=lamb[:].rearrange("p (a c b) -> p a c b", a=2, c=2)[:, :, 1, :], op=ALU.mult),
             reads=["lamb"], writes=["lamjunk"])
        S.op("dve", lambda h: h.tensor_reduce(out=lam_t[:, 0:2], in_=lamjunk[:].rearrange("p (a b) -> p a b", a=2),
                                              axis=mybir.AxisListType.X, op=ALU.add),
             reads=["lamjunk"], writes=["lam0", "lam1"])
        S.op("act", lambda h: h.activation(out=lam_t[:, 2:4], in_=lam_t[:, 0:2], func=AF.Exp), reads=["lam0", "lam1"], writes=["lam2"])
        S.op("dve", lambda h: h.tensor_tensor(out=lam_t[:, 4:5], in0=lam_t[:, 2:3], in1=lam_t[:, 3:4], op=ALU.subtract),
             reads=["lam2"], writes=["lam4"])
        S.op("dve", lambda h: h.tensor_scalar(out=lam_t[:, 5:6], in0=lam_t[:, 4:5], scalar1=LAM_INIT, scalar2=-1.0, op0=ALU.add, op1=ALU.mult),
             reads=["lam4"], writes=["neglam"])
        S.op("dve", lambda h: h.tensor_scalar(out=lam_t[:, 6:7], in0=vec[:, V_SG:V_SG + 1], scalar1=1.0 - LAM_INIT, scalar2=None, op0=ALU.mult),
             reads=["vec"], writes=["sgs"])
        neglam = lam_t[:, 5:6]
        sgs = lam_t[:, 6:7]

        def ada_pp(j, pbank=7):
            for hf in range(2):
                slot = load_wada_half(j, hf)
                def mm(h, slot=slot, hf=hf):
                    ins = None
                    for c4 in range(4):
                        c = hf * 4 + c4
                        for k in range(8):
                            ins = h.matmul(bank(pbank, 2, 2 * c), wa_slots[slot][:, k, c4 * 128:(c4 + 1) * 128], sT[:, k, :],
                                           start=(k == 0), stop=(k == 7))
                    return ins
                S.op("pe", mm, reads=[("wa", slot), "sT"], writes=[PB(pbank)])
            S.op("dve", lambda h: h.tensor_tensor(out=mods[:, j, :, :],
                                                  in0=bank(pbank, 16).rearrange("p (c t) -> p c t", t=2),
                                                  in1=vec[:, V_BADA + j * 8:V_BADA + j * 8 + 8].unsqueeze(2).to_broadcast([128, 8, 2]),
                                                  op=ALU.add),
                 reads=[PB(pbank), "vec"], writes=[("mods", j)])

        def ada_bc(j, pbanks):
            S.dma("sp", [lambda h: h.dma_start(out=bada_bc[:], in_=b_ada[:, j * 1024:(j + 1) * 1024].partition_broadcast(128))],
                  "badabc", writes=["badabc"])
            for n in range(2):
                slot = load_wada_half(j, n)
                for c in range(2):
                    pb = pbanks[c]
                    def mm(h, c=c, slot=slot, pb=pb):
                        ins = None
                        for k in range(8):
                            ins = h.matmul(bank(pb), srep[:, c, k, :], wa_slots[slot][:, k, :], start=(k == 0), stop=(k == 7))
                        return ins
                    S.op("pe", mm, reads=[("wa", slot), ("srep", c)], writes=[PB(pb)])
                    S.op("dve", lambda h, c=c, n=n, pb=pb: h.tensor_tensor(out=gate_t[c][:, n * 512:(n + 1) * 512], in0=bank(pb),
                                                                         in1=bada_bc[:, n * 512:(n + 1) * 512], op=ALU.add),
                         reads=[PB(pb), "badabc"], writes=[("gate", c, n)])

        for i in range(8):
            pass
        def cache_k_prep():
            for hh in range(H):
                r = hh % 2
                S.dma("sp", [lambda h, hh=hh, r=r: h.dma_start(out=ckst_r[r], in_=ck[hh].rearrange("(j p) d -> p j d", p=128))],
                      "ckst%d" % r, writes=[("ckst", r)])
                pb = [6, 7][r]
                def tp(h, r=r, pb=pb):
                    ins = None
                    for j in range(4):
                        ins = h.transpose(bank(pb, 128, j * 128), ckst_r[r][:, j, :], ident[:])
                    return ins
                S.op("pe", tp, reads=[("ckst", r), "ident"], writes=[PB(pb)])
                S.op("act", lambda h, hh=hh, pb=pb: h.activation(out=kT_cache[:, hh, :], in_=bank(pb), func=AF.Copy),
                     reads=[PB(pb)], writes=["kTc", ("kTc", hh)])


        def load_x(i):
            slot = i % NXT
            S.dma("sp", [lambda h: h.dma_start(out=xt[slot], in_=xin[i * 128:(i + 1) * 128, :])], "xt%d" % slot,
                  writes=[("xt", slot)])
            return xt[slot], ("xt", slot)

        for i in range(NXT):
            load_x(i)

        def p1_tiles(b):
            tl = []
            for tt in range(4):
                i = b * 4 + tt
                kx = ("xt", i)
                tl.append((xt[i], kx, xt[i], [kx, kx + ("n",)], kx + ("n",)))
            return tl

        def norm_block(b, tiles, Gm, Gkey, shpart, tag, tp_banks, dst, ex_reads=(), do_stats=True, do_tp=True):
            cnd = 0 if b < 2 else 1
            for tt in (range(4) if do_stats else ()):
                i = b * 4 + tt
                xa, xkey, xn, xnw, xnr = tiles[tt]
                S.op("act", lambda h, xa=xa, i=i: h.activation(out=junk[:], in_=xa, func=AF.Square, accum_out=stat[:, i, 0:1]),
                     reads=[xkey], writes=["junk", ("st0", tag, i)])
                S.op("act", lambda h, i=i: h.activation(out=stat[:, i, 1:2], in_=stat[:, i, 0:1], func=AF.Ln,
                                                       scale=1.0 / D, bias=epsc[:]),
                     reads=[("st0", tag, i), "epsc"], writes=[("st1", tag, i)])
                S.op("act", lambda h, i=i: h.activation(out=stat[:, i, 2:3], in_=stat[:, i, 1:2], func=AF.Exp, scale=-0.5),
                     reads=[("st1", tag, i)], writes=[("st2", tag, i)])
                S.op("dve", lambda h, xa=xa, xn=xn, i=i: h.tensor_scalar(out=xn, in0=xa, scalar1=stat[:, i, 2:3], scalar2=None, op0=ALU.mult),
                     reads=[xkey, ("st2", tag, i)], writes=xnw)
            for c in (range(8) if do_tp else ()):
                pb = tp_banks[c % len(tp_banks)]
                def tp(h, c=c, pb=pb):
                    ins = None
                    for tt in range(4):
                        ins = h.transpose(bank(pb, 128, tt * 128), tiles[tt][2][:, c * 128:(c + 1) * 128], ident[:])
                    return ins
                S.op("pe", tp, reads=[t[4] for t in tiles] + ["ident"], writes=[PB(pb)])
                wk = [("hT", c, b)] + ([("mixp", c, 0), ("mixp", c, 1)] if b == 2 else [])
                if c % 2 == 0:
                    S.op("dve", lambda h, c=c, pb=pb: h.tensor_scalar(
                        out=dst[:, c, b * 512:(b + 1) * 512], in0=bank(pb), scalar1=Gm[:, c, cnd:cnd + 1],
                        scalar2=mods[:, shpart, c, cnd:cnd + 1], op0=ALU.mult, op1=ALU.add),
                        reads=[PB(pb), Gkey, ("mods", shpart)] + list(ex_reads), writes=wk)
                else:
                    S.op("act", lambda h, c=c, pb=pb: h.activation(
                        out=dst[:, c, b * 512:(b + 1) * 512], in_=bank(pb), func=AF.Identity,
                        scale=Gm[:, c, cnd:cnd + 1], bias=mods[:, shpart, c, cnd:cnd + 1]),
                        reads=[PB(pb), Gkey, ("mods", shpart)] + list(ex_reads), writes=wk)

        for b in range(3):
            norm_block(b, p1_tiles(b), G1, "G1", 0, "hT", [0, 1, 2, 3], hT, do_tp=False)
        ada_pp(0)
        ada_pp(1)
        wa_extra = S.retire([("wa", 2), ("wa", 3)])
        for kk in [("kst", 0), ("kst", 1), ("vst", 0, 0), ("vst", 0, 1), ("vst", 1, 0), ("vst", 1, 1), "ksq0", "ksq1", ("ckst", 0), ("ckst", 1)]:
            S.seed(kk, wa_extra)
        S.op("dve", lambda h: h.scalar_tensor_tensor(out=G1[:], in0=mods[:, 1, :, :], scalar=1.0,
                                                     in1=vec[:, V_N1:V_N1 + 8].unsqueeze(2).to_broadcast([128, 8, 2]),
                                                     op0=ALU.add, op1=ALU.mult),
             reads=[("mods", 1), "vec"], writes=["G1"])
        for b in range(3):
            norm_block(b, p1_tiles(b), G1, "G1", 0, "hT", [0, 1, 2, 3], hT, do_stats=False)
        xt_free = S.retire([("xt", s_) for s_ in range(NXT)] + [("xt", s_, "n") for s_ in range(NXT)])
        for i in range(12):
            S.seed(("V", i), xt_free)
            S.seed(("V", i, 1), xt_free)
        S.seed(("Vc", 0), xt_free)
        S.seed("kTc", xt_free)
        for hh_ in range(H):
            for b_ in range(3):
                S.seed(("kT", hh_, b_), xt_free)

        if debug == "h":
            dbg["hT"] = dout("dbg_hT", [128, 8, NTOK], BF16)
            S.dma("sp", [lambda h: h.dma_start(out=dbg["hT"], in_=hT)], "dbg",
                  reads=[("hT", c, b) for c in range(8) for b in range(3)], is_output=True)
            S.finish()
            return nc

        S.dma("sp", [lambda h: h.dma_start(out=ropeT, in_=c_rope)], "rope", writes=["rope"])
        S.dma("sp", [lambda h: h.dma_start(out=dftc, in_=c_dftc)], "dftc", writes=["dftc"])
        S.dma("sp", [lambda h: h.dma_start(out=dftp, in_=c_dftp)], "dftp", writes=["dftp"])
        cosT = ropeT[:, 0, :]
        sinT = ropeT[:, 1, :]

        def load_win(g):
            slot = g % 3
            S.dma("pool", [lambda h, k=k: h.dma_start(out=win_slots[slot][:, k, :], in_=win_v[:, k, g * 512:(g + 1) * 512])
                           for k in range(8)], "win%d" % slot, writes=[("win", slot)])

        load_win(0)
        load_win(1)
        load_win(2)

        def hT_keys(b):
            return [("hT", c, b) for c in range(8)]

        ucount = [0]

        def qk_unit(m, b):
            u = ucount[0]
            ucount[0] += 1
            g, mc = m // 4, m % 4
            slot = g % 3
            pj = [0, 1, 2][u % 3]
            st = {}

            def proj():
                def mm(h):
                    ins = None
                    for k in range(8):
                        ins = h.matmul(bank(pj), win_slots[slot][:, k, mc * 128:(mc + 1) * 128], hT[:, k, b * 512:(b + 1) * 512],
                                       start=(k == 0), stop=(k == 7))
                    return ins
                S.op("pe", mm, reads=[("win", slot)] + hT_keys(b), writes=[PB(pj)])
            st["proj"] = proj
            if m < 2:
                def cp():
                    S.op("act", lambda h: h.activation(out=xfT[:, m, b * 512:(b + 1) * 512], in_=bank(pj), func=AF.Copy),
                         reads=[PB(pj)], writes=[("xfT", m, b)])
                st["sq"] = cp
                return st
            isq = m < 8
            hh = (m - 2) if isq else (m - 8)
            gcol = vec[:, V_QG:V_QG + 1] if isq else vec[:, V_KG:V_KG + 1]
            dstT = qT if isq else kT_own
            dkey = ("qT" if isq else "kT", hh, b)
            r = u % 2
            msb = [3, 4][u % 2]
            rtb = 5
            sq, lnb, qn, t1, qnb = sq_r[r], ln_r[r], qn_r[r], t1_r[r], qnb_r[r]

            def s_sq():
                S.op("act", lambda h: h.activation(out=sq, in_=bank(pj), func=AF.Square), reads=[PB(pj)], writes=[("sq", r)])
            def s_ms():
                S.op("pe", lambda h: h.matmul(bank(msb), blk64, sq, start=True, stop=True), reads=[("sq", r), "cbf"], writes=[PB(msb)])
            def s_ln():
                S.op("act", lambda h: h.activation(out=lnb, in_=bank(msb), func=AF.Ln, bias=epsc[:]),
                     reads=[PB(msb), "epsc"], writes=[("ln", r)])
            def s_exp():
                S.op("act", lambda h: h.activation(out=lnb, in_=lnb, func=AF.Exp, scale=-0.5), reads=[("ln", r)], writes=[("ln", r)])
            if b == 2:
                def s_stt():
                    S.op("dve", lambda h: h.scalar_tensor_tensor(out=dstT[:, hh, b * 512:(b + 1) * 512], in0=bank(pj), scalar=gcol, in1=lnb,
                                                                 op0=ALU.mult, op1=ALU.mult),
                         reads=[PB(pj), ("ln", r), "vec"], writes=[dkey])
                st["sq"] = s_sq
                st["c"] = lambda: [f() for f in (s_ms, s_ln, s_exp, s_stt)]
                return st
            def s_stt():
                S.op("dve", lambda h: h.scalar_tensor_tensor(out=qn, in0=bank(pj), scalar=gcol, in1=lnb, op0=ALU.mult, op1=ALU.mult),
                     reads=[PB(pj), ("ln", r), "vec"], writes=[("qn", r)])
            def s_cast():
                S.op("act", lambda h: h.activation(out=qnb, in_=qn, func=AF.Copy), reads=[("qn", r)], writes=[("qnb", r)])
            def s_rot():
                S.op("pe", lambda h: h.matmul(bank(rtb), perm, qnb, start=True, stop=True), reads=[("qnb", r), "cbf"], writes=[PB(rtb)])
            def s_t1():
                S.op("dve", lambda h: h.tensor_tensor(out=t1, in0=qn, in1=cosT[:, b * 512:(b + 1) * 512], op=ALU.mult),
                     reads=[("qn", r), "rope"], writes=[("t1", r)])
            def s_t2():
                S.op("dve", lambda h: h.tensor_tensor(out=qn, in0=bank(rtb), in1=sinT[:, b * 512:(b + 1) * 512], op=ALU.mult),
                     reads=[PB(rtb), "rope"], writes=[("qn", r)])
            def s_add():
                S.op("dve", lambda h: h.tensor_tensor(out=dstT[:, hh, b * 512:(b + 1) * 512], in0=t1, in1=qn, op=ALU.add),
                     reads=[("t1", r), ("qn", r)], writes=[dkey])
            st["sq"] = s_sq
            st["c"] = lambda: [f() for f in (s_ms, s_ln, s_exp, s_stt, s_t1)]
            st["cast"] = s_cast
            st["d"] = lambda: [f() for f in (s_rot, s_t2, s_add)]
            return st

        def pipeline_qk(units):
            n = len(units)
            def call(i, nm):
                if 0 <= i < n and nm in units[i]:
                    units[i][nm]()
            for t in range(n + 4):
                call(t - 4, "d")
                call(t - 2, "c")
                call(t - 1, "sq")
                call(t - 2, "cast")
                call(t, "proj")

        kstat = sb("kstat", [128, 2, 12])
        def kdup_tile(ti, B6, B7):
            i = 8 + ti
            r = ti % 2
            kst = kst_r[r]
            def mm(h, i=i, B6=B6, B7=B7):
                ins = None
                for k in range(8):
                    ins = h.matmul(bank(B6), hT[:, k, i * 128:(i + 1) * 128], win_slots[2][:, k, :], start=(k == 0), stop=(k == 7))
                for k in range(8):
                    ins = h.matmul(bank(B7, 256), hT[:, k, i * 128:(i + 1) * 128], win_slots[0][:, k, 0:256], start=(k == 0), stop=(k == 7))
                return ins
            S.op("pe", mm, reads=[("win", 2), ("win", 0)] + hT_keys(2), writes=[PB(B6), PB(B7)])
            S.op("act", lambda h, B6=B6: h.activation(out=ksq[:, 0:512], in_=bank(B6), func=AF.Square), reads=[PB(B6)], writes=["ksq0"])
            S.op("act", lambda h, B7=B7: h.activation(out=ksq[:, 512:768], in_=bank(B7, 256), func=AF.Square), reads=[PB(B7)], writes=["ksq1"])
            S.op("dve", lambda h: h.tensor_reduce(out=kstat[:, 0, :], in_=ksq.rearrange("p (g d) -> p g d", d=64),
                                                  axis=mybir.AxisListType.X, op=ALU.add),
                 reads=["ksq0", "ksq1"], writes=["kstat0"])
            S.op("act", lambda h: h.activation(out=kstat[:, 1, :], in_=kstat[:, 0, :], func=AF.Ln, scale=1.0 / 64, bias=epsc[:]),
                 reads=["kstat0", "epsc"], writes=["kstat1"])
            S.op("act", lambda h: h.activation(out=kstat[:, 1, :], in_=kstat[:, 1, :], func=AF.Exp, scale=-0.5),
                 reads=["kstat1"], writes=["kstat1"])
            S.op("dve", lambda h, kst=kst, B6=B6: h.tensor_tensor(out=kst[:, 0:512].rearrange("p (g d) -> p g d", d=64),
                                                           in0=bank(B6).rearrange("p (g d) -> p g d", d=64),
                                                           in1=kstat[:, 1, 0:8].unsqueeze(2).to_broadcast([128, 8, 64]), op=ALU.mult),
                 reads=[PB(B6), "kstat1"], writes=[("kst", r)])
            S.op("dve", lambda h, kst=kst, B7=B7: h.tensor_tensor(out=kst[:, 512:768].rearrange("p (g d) -> p g d", d=64),
                                                           in0=bank(B7, 256).rearrange("p (g d) -> p g d", d=64),
                                                           in1=kstat[:, 1, 8:12].unsqueeze(2).to_broadcast([128, 4, 64]), op=ALU.mult),
                 reads=[PB(B7), "kstat1"], writes=[("kst", r)])
            S.op("dve", lambda h, kst=kst: h.tensor_tensor(out=kst.rearrange("p (g d) -> p g d", d=64),
                                                           in0=kst.rearrange("p (g d) -> p g d", d=64),
                                                           in1=qkg_bc[:, 64:128].unsqueeze(1).to_broadcast([128, 12, 64]), op=ALU.mult),
                 reads=[("kst", r), "qkg"], writes=[("kst", r)])
            pi, tt = ti // 2, ti % 2
            S.dma("sp", [lambda h, kst=kst, pi=pi, tt=tt: h.dma_start(
                out=nk[pi, :, tt * 128:(tt + 1) * 128, :].rearrange("h s d -> s h d"),
                in_=kst.rearrange("p (h d) -> p h d", d=128))], "kst%d" % r,
                reads=[("kst", r)], is_output=True)

        def v_tile(i, B6, B7):
            b = i // 4
            def mm(h, i=i, B6=B6, B7=B7):
                ins = None
                for k in range(8):
                    ins = h.matmul(bank(B6, 256), hT[:, k, i * 128:(i + 1) * 128], win_slots[0][:, k, 256:512], start=(k == 0), stop=(k == 7))
                for k in range(8):
                    ins = h.matmul(bank(B7), hT[:, k, i * 128:(i + 1) * 128], win_slots[1][:, k, :], start=(k == 0), stop=(k == 7))
                return ins
            S.op("pe", mm, reads=[("win", 0), ("win", 1)] + hT_keys(b), writes=[PB(B6), PB(B7)])
            if i < 8:
                S.op("act", lambda h, i=i, B6=B6: h.activation(out=V_own[:, i, 0:256], in_=bank(B6, 256), func=AF.Copy),
                     reads=[PB(B6)], writes=[("V", i)])
                S.op("dve", lambda h, i=i, B7=B7: h.tensor_copy(out=V_own[:, i, 256:768], in_=bank(B7)),
                     reads=[PB(B7)], writes=[("V", i, 1)])
            else:
                r = i % 2
                vst = vst_r[r]
                S.op("act", lambda h, vst=vst, B6=B6: h.activation(out=vst[:, 0:256], in_=bank(B6, 256), func=AF.Copy),
                     reads=[PB(B6)], writes=[("vst", r, 0)])
                S.op("dve", lambda h, vst=vst, B7=B7: h.tensor_copy(out=vst[:, 256:768], in_=bank(B7)),
                     reads=[PB(B7)], writes=[("vst", r, 1)])
                S.op("dve", lambda h, vst=vst, i=i: h.tensor_copy(out=V_own[:, i, :], in_=vst),
                     reads=[("vst", r, 0), ("vst", r, 1)], writes=[("V", i), ("V", i, 1)])
                pi, tt = (i - 8) // 2, (i - 8) % 2
                S.dma("sp", [lambda h, vst=vst, pi=pi, tt=tt: h.dma_start(
                    out=nv[pi, :, tt * 128:(tt + 1) * 128, :].rearrange("h s d -> s h d"),
                    in_=vst.rearrange("p (h d) -> p h d", d=128))], "vst%d" % r,
                    reads=[("vst", r, 0), ("vst", r, 1)], is_output=True)

        units = []
        for m in range(0, 14):
            for b in range(3):
                units.append(qk_unit(m, b))
            if m == 3:
                units.append({"proj": lambda: ada_bc(2, [6, 7])})
            if m == 5:
                units.append({"proj": lambda: load_win(3)})
            if m == 6:
                units.append({"proj": lambda: ada_pp(3, 6)})
            if m == 9:
                units.append({"proj": lambda: ada_pp(4, 7)})
                units.append({"proj": lambda: load_win(4)})
            if m == 11:
                units.append({"proj": cache_k_prep})
        for i_ in (8, 9, 10, 11):
            units.append({"proj": lambda i_=i_: v_tile(i_, 6, 7)})
        pipeline_qk(units)
        dfts = carve(A1, [128, 8, 1024], BF16)
        S.seed("dfts", S.retire([("wa", 0), ("wa", 1)]))
        S.dma("sp", [lambda h, k=k: h.dma_start(out=dfts[:, k, :], in_=c_dfts[:, k, :]) for k in range(8)], "dfts", writes=["dfts"])

        tm_pairs = [(6, 7), (3, 4), (0, 1), (2, 5)]
        tm_seq = [("k", 0), ("v", 0), ("v", 1), ("k", 1), ("v", 2), ("k", 2), ("v", 3), ("v", 4), ("k", 3),
                  ("v", 5), ("v", 6), ("v", 7)]
        for n_, (kind_, idx_) in enumerate(tm_seq):
            B6_, B7_ = tm_pairs[n_ % 4]
            if kind_ == "k":
                kdup_tile(idx_, B6_, B7_)
            else:
                v_tile(idx_, B6_, B7_)

        S.dma("pool", [lambda h, hh=hh: h.dma_start(out=V_cache[:, :, hh * 128:(hh + 1) * 128],
                                                    in_=cv[hh].rearrange("(j p) d -> p j d", p=128)) for hh in range(H)],
              "vcache", writes=[("Vc", 0)])
        if debug == "qkv":
            for nm, t_, shp, keys in [("qT", qT, [128, 6, NTOK], [("qT", hh, b) for hh in range(6) for b in range(3)]),
                                      ("kT", kT_own, [128, 6, NTOK], [("kT", hh, b) for hh in range(6) for b in range(3)]),
                                      ("xfT", xfT, [128, 2, NTOK], [("xfT", m, b) for m in range(2) for b in range(3)]),
                                      ("V", V_own, [128, 12, 768], [("V", i) for i in range(12)] + [("V", i, 1) for i in range(12)]),
                                      ("Vc", V_cache, [128, 4, 768], [("Vc", 0)]),
                                      ("kTc", kT_cache, [128, 6, 512], [("kTc", hh) for hh in range(6)])]:
                dbg[nm] = dout("dbg_" + nm, shp, BF16)
                S.dma("sp", [lambda h, nm=nm, t_=t_: h.dma_start(out=dbg[nm], in_=t_)], "dbg_" + nm, reads=keys, is_output=True)
            S.finish()
            return nc

        bt = S.all_tokens()
        o3 = P23_END
        o3 += 32768
        o3 += 2048 + 512
        u_tok = carve(o3, [128, 12, 512], BF16); o3 += 12288
        E_r = [carve(o3 + i * 2048, [128, 2, 512], BF16) for i in range(3)]; o3 += 6144
        zc = [carve(o3 + i * 2048, [128, 512]) for i in range(2)]; o3 += 4096
        oc = [carve(o3 + i * 2048, [128, 512]) for i in range(2)]; o3 += 4096
        of = carve(o3, [128, 512]); o3 += 2048
        osq = carve(o3, [128, 512], BF16); o3 += 1024
        oln = carve(o3, [128, 512]); o3 += 2048
        assert o3 <= ARENA_BYTES
        for kk in ["of", "osq", "oln", "ualt"] + [("utok", i) for i in range(12)] + [("E", i) for i in range(3)] \
                + [("zc", i) for i in range(2)] + [("oc", i) for i in range(2)]:
            S.seed(kk, bt)

        S.op("dve", lambda h: h.scalar_tensor_tensor(out=G2[:], in0=mods[:, 4, :, :], scalar=1.0,
                                                     in1=vec[:, V_N2:V_N2 + 8].unsqueeze(2).to_broadcast([128, 8, 2]),
                                                     op0=ALU.add, op1=ALU.mult),
             reads=[("mods", 4), "vec"], writes=["G2"])

        for i in range(12):
            pb = i % 4
            b = i // 4
            def mm(h, i=i, pb=pb):
                ins = None
                for fc in range(2):
                    ins = h.matmul(bank(pb, 256, fc * 256), xfT[:, fc, i * 128:(i + 1) * 128], dftc, start=True, stop=True)
                return ins
            S.op("pe", mm, reads=[("xfT", 0, b), ("xfT", 1, b), "dftc"], writes=[PB(pb)])
            if i % 2 == 0:
                S.op("act", lambda h, i=i, pb=pb: h.activation(out=u_tok[:, i, :], in_=bank(pb), func=AF.Copy),
                     reads=[PB(pb)], writes=[("utok", i)])
            else:
                S.op("dve", lambda h, i=i, pb=pb: h.tensor_copy(out=u_tok[:, i, :], in_=bank(pb)),
                     reads=[PB(pb)], writes=[("utok", i)])
        fcount = [0]
        def fourier2(tiles, table, tkey, ns, tok0, ncols_blocks, wkeys_fn):
            for fc in range(2):
                for tb, (c0, ncol) in enumerate(ncols_blocks):
                    pb = fcount[0] % 4
                    fcount[0] += 1
                    def mm(h, fc=fc, c0=c0, ncol=ncol, pb=pb):
                        ins = None
                        n = len(tiles) * 2
                        idx = 0
                        for jl, j in enumerate(tiles):
                            for cs in range(2):
                                ins = h.matmul(bank(pb, ncol), u_tok[:, j, fc * 256 + cs * 128: fc * 256 + cs * 128 + 128],
                                               table[:, jl, cs * ns + c0: cs * ns + c0 + ncol], start=(idx == 0), stop=(idx == n - 1))
                                idx += 1
                        return ins
                    S.op("pe", mm, reads=[("utok", j) for j in tiles] + [tkey], writes=[PB(pb)])
                    wk = wkeys_fn(fc, tb)
                    if fcount[0] % 2 == 0:
                        S.op("act", lambda h, fc=fc, c0=c0, ncol=ncol, pb=pb: h.activation(
                            out=mixT[:, fc, tok0 + c0: tok0 + c0 + ncol], in_=bank(pb, ncol), func=AF.Copy), reads=[PB(pb)], writes=wk)
                    else:
                        S.op("dve", lambda h, fc=fc, c0=c0, ncol=ncol, pb=pb: h.tensor_copy(
                            out=mixT[:, fc, tok0 + c0: tok0 + c0 + ncol], in_=bank(pb, ncol)), reads=[PB(pb)], writes=wk)
        fourier2([8, 9], dftp, "dftp", 256, 1024, [(0, 256)], lambda fc, tb: [("hT", fc, 2), ("mixp", fc, 0)])
        fourier2([10, 11], dftp, "dftp", 256, 1280, [(0, 256)], lambda fc, tb: [("hT", fc, 2), ("mixp", fc, 1)])
        u_alt = carve(P23_END + 22528, [128, 8, 512], BF16)
        S.op("dve", lambda h: h.tensor_scalar(out=u_alt.rearrange("p a b -> p (a b)"), in0=u_tok[:, 0:8, :].rearrange("p a b -> p (a b)"),
                                              scalar1=vec[:, V_SIGN:V_SIGN + 1], scalar2=None, op0=ALU.mult),
             reads=[("utok", i) for i in range(8)] + ["vec"], writes=["ualt"])
        for fc in range(2):
            for tb in range(2):
                pb = fcount[0] % 4
                fcount[0] += 1
                usrc = u_tok if tb == 0 else u_alt
                def mm(h, fc=fc, pb=pb, usrc=usrc):
                    ins = None
                    idx = 0
                    for j in range(8):
                        for cs in range(2):
                            ins = h.matmul(bank(pb), usrc[:, j, fc * 256 + cs * 128: fc * 256 + cs * 128 + 128],
                                           dfts[:, j, cs * 512: cs * 512 + 512], start=(idx == 0), stop=(idx == 15))
                            idx += 1
                    return ins
                S.op("pe", mm, reads=[("utok", j) for j in range(8)] + ["dfts", "ualt"], writes=[PB(pb)])
                if fcount[0] % 2 == 0:
                    S.op("act", lambda h, fc=fc, tb=tb, pb=pb: h.activation(out=mixT[:, fc, tb * 512:(tb + 1) * 512], in_=bank(pb), func=AF.Copy),
                         reads=[PB(pb)], writes=[("hT", fc, tb)])
                else:
                    S.op("dve", lambda h, fc=fc, tb=tb, pb=pb: h.tensor_copy(out=mixT[:, fc, tb * 512:(tb + 1) * 512], in_=bank(pb)),
                         reads=[PB(pb)], writes=[("hT", fc, tb)])
        wout = carve(A1, [128, 8, D], BF16)
        wout_v = w_out.rearrange("(k p) n -> p k n", p=128)
        S.seed("wout", S.retire(["dfts"]))
        S.dma("pool", [lambda h, k=k: h.dma_start(out=wout[:, k, :], in_=wout_v[:, k, :]) for k in range(8)], "wout", writes=["wout"])

        x1 = [carve(A2 + 98304 + i * 4096, [128, D]) for i in range(6)] + [carve(A2 + i * 4096, [128, D]) for i in range(6)]
        assert A2 + 98304 >= P23_END + 19968 and A2 + 98304 + 6 * 4096 <= P23_END + 47616
        btf = S.all_tokens()
        for i in range(6):
            S.seed(("x1", i), btf)
            S.dma("sp", [lambda h, i=i: h.dma_start(out=x1[i], in_=xin[i * 128:(i + 1) * 128, :])], "x1_%d" % i, writes=[("x1", i)])
        ft = bt
        oP = P23_END
        NPS = 3
        zcP = [[carve(oP + (c * NPS + s_) * 1024, [128, 256]) for s_ in range(NPS)] for c in range(2)]; oP += 2 * NPS * 1024
        ocP = [[carve(oP + (c * NPS + s_) * 1024, [128, 256]) for s_ in range(NPS)] for c in range(2)]; oP += 2 * NPS * 1024
        ofP = [carve(oP + s_ * 1024, [128, 256]) for s_ in range(NPS)]; oP += NPS * 1024
        olnP = [carve(oP + s_ * 1024, [128, 256]) for s_ in range(NPS)]; oP += NPS * 1024
        osqP = [carve(oP + s_ * 512, [128, 256], BF16) for s_ in range(NPS)]; oP += NPS * 512
        assert oP <= P23_END + 32768
        for s_ in range(NPS):
            for kk in [("zcP", 0, s_), ("zcP", 1, s_), ("ocP", 0, s_), ("ocP", 1, s_), ("ofP", s_), ("olnP", s_), ("osqP", s_)]:
                S.seed(kk, ft)
        pairs = [(0, 1), (2, 3)]
        acount = [0]
        gcount = [0]

        pgi = [0]

        def attn_group(hh, q0, nq, ktiles, wkeys):
            units = []
            nkt = len(ktiles)
            ozo = 0
            if nq == 256:
                ozo = 0
                pgi[0] += 1
            if nq == 256:
                bO, bZ = (4, 5) if pgi[0] % 2 == 1 else (6, 7)
                OZK = [PB(bO), PB(bZ)]
                def OB(b_):
                    return {4: bank(bO, 256, 0), 5: bank(bO, 256, 256), 6: bank(bZ, 256, 0), 7: bank(bZ, 256, 256)}[b_]
                def OK_(b_):
                    return [PB(bO)] if b_ in (4, 5) else [PB(bZ)]
            else:
                bO = bZ = None
                OZK = [PB(4), PB(5), PB(6), PB(7)]
                def OB(b_):
                    return bank(b_, nq)
                def OK_(b_):
                    return [PB(b_)]
            for ti, (k_ap, v_ap, kkeys) in enumerate(ktiles):
                u = acount[0]
                acount[0] += 1
                pr = pairs[u % 2]
                er = u % 3
                E = E_r[er]
                first, last = ti == 0, ti == nkt - 1
                qb = q0 // 512

                def sA(k_ap=k_ap, pr=pr, kkeys=kkeys):
                    def mm(h):
                        h.matmul(bank(pr[0], nq), k_ap[0:64, :], qT[0:64, hh, q0:q0 + nq], start=True, stop=True)
                        return h.matmul(bank(pr[1], nq), k_ap[64:128, :], qT[64:128, hh, q0:q0 + nq], start=True, stop=True)
                    S.op("pe", mm, reads=[("qT", hh, qb)] + [kk for kk in kkeys if kk[0] in ("kT", "kTc")], writes=[PB(pr[0]), PB(pr[1])])

                def sB(pr=pr, E=E, er=er):
                    Eo = E[:, :, 0:nq] if nq == 512 else E.rearrange("p c n -> p (c n)")[:, 0:512].rearrange("p (c n) -> p c n", c=2)
                    S.op("act", lambda h: h.activation(
                        out=Eo,
                        in_=ps_all[:, pr[0] * 512: pr[0] * 512 + 1024].rearrange("p (c n) -> p c n", c=2)[:, :, 0:nq],
                        func=AF.Exp, scale=0.125), reads=[PB(pr[0]), PB(pr[1])], writes=[("E", er)])

                def sC(v_ap=v_ap, E=E, er=er, first=first, last=last, kkeys=kkeys, u=u):
                    def mm(h):
                        if nq == 256:
                            Ef = E.rearrange("p c n -> p (c n)")[:, 0:512]
                            h.matmul(bank(bO), v_ap, Ef, start=first, stop=last)
                            return h.matmul(bank(bZ), ones_bf, Ef, start=first, stop=last)
                        h.matmul(OB(4), v_ap, E[:, 0, 0:nq], start=first, stop=last)
                        h.matmul(OB(5), v_ap, E[:, 1, 0:nq], start=first, stop=last)
                        h.matmul(OB(6), ones_bf, E[:, 0, 0:nq], start=first, stop=last)
                        return h.matmul(OB(7), ones_bf, E[:, 1, 0:nq], start=first, stop=last)
                    S.op("pe", mm, reads=[("E", er), "cbf"] + [kk for kk in kkeys if kk[0] in ("V", "Vc")],
                         writes=OZK)
                    if last:
                        g = gcount[0]
                        gcount[0] += 1
                        slot = g % NPS if nq == 256 else 0
                        fc, fc1, fc2 = fin_ab(slot)
                        if nq == 256:
                            e2 = [4, fc2, None]
                            deferred.append([3, fc1, e2])
                            deferred.append(e2)
                        else:
                            deferred.append([5, fc, None])
                    for it in list(deferred):
                        if it not in deferred:
                            continue
                        if it[0] <= 0:
                            deferred.remove(it)
                            it[1](u, None if nq == 512 else (6 if bO == 4 else 4))
                            if nq == 512 and it[2] is not None and it[2] in deferred:
                                deferred.remove(it[2])
                                it[2][1](u, None)
                        else:
                            it[0] -= 1

                def fin_ab(slot=0):
                    if nq == 256:
                        Z0, Z1_, O0, O1_, OF, OSQ, OLN = zcP[0][slot], zcP[1][slot], ocP[0][slot], ocP[1][slot], ofP[slot], osqP[slot], olnP[slot]
                        kz0, kz1, ko0, ko1, kof, kosq, koln = [("zcP", 0, slot)], [("zcP", 1, slot)], [("ocP", 0, slot)], [("ocP", 1, slot)], \
                            [("ofP", slot)], [("osqP", slot)], [("olnP", slot)]
                    else:
                        Z0, Z1_, O0, O1_, OF, OSQ, OLN = zc[0], zc[1], oc[0], oc[1], of, osq, oln
                        kz0, kz1, ko0, ko1, kof, kosq, koln = [("zc", 0)], [("zc", 1)], [("oc", 0)], [("oc", 1)], ["of"], ["osq"], ["oln"]
                    if nq == 256:
                        S.op("act", lambda h: h.activation(out=Z0, in_=OB(6), func=AF.Ln), reads=OK_(6), writes=kz0)
                        S.op("act", lambda h: h.activation(out=Z1_, in_=OB(7), func=AF.Ln), reads=OK_(7), writes=kz1)
                        S.op("act", lambda h: h.activation(out=Z0, in_=Z0, func=AF.Exp, scale=-1.0), reads=kz0, writes=kz0)
                        S.op("act", lambda h: h.activation(out=Z1_, in_=Z1_, func=AF.Exp, scale=-1.0), reads=kz1, writes=kz1)
                        S.op("dve", lambda h: h.tensor_tensor(out=O0, in0=OB(4), in1=Z0, op=ALU.mult), reads=OK_(4) + kz0, writes=ko0)
                        S.op("dve", lambda h: h.tensor_tensor(out=O1_, in0=OB(5), in1=Z1_, op=ALU.mult), reads=OK_(5) + kz1, writes=ko1)
                    else:
                        S.op("dve", lambda h: h.tensor_copy(out=Z0, in_=OB(6)), reads=OK_(6), writes=kz0)
                        S.op("act", lambda h: h.activation(out=Z1_, in_=OB(7), func=AF.Ln), reads=OK_(7), writes=kz1)
                        S.op("act", lambda h: h.activation(out=O0, in_=OB(4), func=AF.Copy), reads=OK_(4), writes=ko0)
                        S.op("dve", lambda h: h.tensor_copy(out=O1_, in_=OB(5)), reads=OK_(5), writes=ko1)
                        S.op("dve", lambda h: h.reciprocal(out=Z0, in_=Z0), reads=kz0, writes=kz0)
                        S.op("act", lambda h: h.activation(out=Z1_, in_=Z1_, func=AF.Exp, scale=-1.0), reads=kz1, writes=kz1)
                        S.op("dve", lambda h: h.tensor_tensor(out=O0, in0=O0, in1=Z0, op=ALU.mult), reads=ko0 + kz0, writes=ko0)
                        S.op("dve", lambda h: h.tensor_tensor(out=O1_, in0=O1_, in1=Z1_, op=ALU.mult), reads=ko1 + kz1, writes=ko1)
                    S.op("dve", lambda h: h.scalar_tensor_tensor(out=OF, in0=O1_, scalar=neglam, in1=O0, op0=ALU.mult, op1=ALU.add),
                         reads=ko0 + ko1 + ["neglam"], writes=kof)

                    st_ = {}

                    def fin_c1(ucur, free_bank=None):
                        mpr = pairs[(ucur + 1) % 2] if free_bank is None else (free_bank, free_bank)
                        st_["mpr"] = mpr
                        S.op("act", lambda h: h.activation(out=OSQ, in_=OF, func=AF.Square), reads=kof, writes=kosq)
                        S.op("pe", lambda h: h.matmul(bank(mpr[0], nq), mean128, OSQ, start=True, stop=True),
                             reads=kosq + ["cbf"], writes=[PB(mpr[0]), PB(mpr[1])])

                    def fin_c2(ucur, free_bank=None):
                        mpr = st_["mpr"]
                        S.op("act", lambda h: h.activation(out=OLN, in_=bank(mpr[0], nq), func=AF.Ln, bias=epsc[:]),
                             reads=[PB(mpr[0]), "epsc"], writes=koln)
                        S.op("act", lambda h: h.activation(out=OLN, in_=OLN, func=AF.Exp, scale=-0.5), reads=koln, writes=koln)
                        S.op("dve", lambda h: h.scalar_tensor_tensor(out=mixT[:, 2 + hh, q0:q0 + nq], in0=OF, scalar=sgs, in1=OLN,
                                                                     op0=ALU.mult, op1=ALU.mult),
                             reads=kof + koln + ["sgs"], writes=wkeys)

                    def fin_c(ucur, free_bank=None):
                        fin_c1(ucur, free_bank)
                        fin_c2(ucur, free_bank)
                    return fin_c, fin_c1, fin_c2

                units.append([sA, sB, sC, (nq == 512 and first)])
            return units

        all_units = []
        deferred = []
        sgroups, pgroups = [], []
        for p_ in range(2):
            for hh in range(H):
                kts = []
                for j in (8 + 2 * p_, 9 + 2 * p_):
                    kts.append((kT_own[:, hh, j * 128:(j + 1) * 128], V_own[:, j, hh * 128:(hh + 1) * 128],
                                [("kT", hh, 2), ("V", j), ("V", j, 1)]))
                pgroups.append((hh, 1024 + 256 * p_, 256, kts, [("hT", 2 + hh, 2), ("mixp", 2 + hh, p_)]))
        for qb in range(2):
            for hh in range(H):
                kts = []
                for j in range(8):
                    kts.append((kT_own[:, hh, j * 128:(j + 1) * 128], V_own[:, j, hh * 128:(hh + 1) * 128],
                                [("kT", hh, j // 4), ("V", j), ("V", j, 1)]))
                for j in range(4):
                    kts.append((kT_cache[:, hh, j * 128:(j + 1) * 128], V_cache[:, j, hh * 128:(hh + 1) * 128],
                                [("kTc", hh), ("Vc", 0)]))
                sgroups.append((hh, qb * 512, 512, kts, [("hT", 2 + hh, qb)]))
        for gi_ in range(12):
            all_units += attn_group(*pgroups[gi_])
        for gi_ in range(12):
            all_units += attn_group(*sgroups[gi_])
        nu_ = len(all_units)
        cstep = {}
        for u_ in range(nu_):
            cstep.setdefault(u_ + (3 if all_units[u_][3] else 2), []).append(u_)
        for t_ in range(nu_ + 4):
            if t_ < nu_:
                all_units[t_][0]()
            if 0 <= t_ - 1 < nu_:
                all_units[t_ - 1][1]()
            for u_ in sorted(cstep.get(t_, [])):
                all_units[u_][2]()

        def flush_attention():
            for it in deferred:
                it[1](acount[0] - 1, None)
            del deferred[:]

        def mix_keys(b):
            ks = [("hT", c, b) for c in range(8)]
            if b == 2:
                ks += [("mixp", c, p_) for c in range(8) for p_ in range(2)]
            return ks

        if debug == "mix":
            flush_attention()
            dbg["mixT"] = dout("dbg_mixT", [128, 8, NTOK], BF16)
            S.dma("sp", [lambda h: h.dma_start(out=dbg["mixT"], in_=mixT)], "dbg", reads=mix_keys(0) + mix_keys(1) + mix_keys(2), is_output=True)
            S.finish()
            return nc

        bt4 = S.all_tokens()
        wg_r = [carve(A2 + 24576 + i * 8192, [128, 8, 512], BF16) for i in range(2)]
        wu_r = [carve(A2 + 40960 + i * 8192, [128, 8, 512], BF16) for i in range(2)]
        wd = carve(A2 + 57344, [128, 8, D], BF16)
        uT = carve(A2 + 73728, [128, 8, NTOK], BF16)
        xn5 = [carve(A2 + 73728 + i * 4096, [128, D]) for i in range(4)]
        gsb = carve(A2 + 122880, [128, NTOK])
        tmp4 = [carve(A2 + 122880 + i * 2048, [128, 512]) for i in range(2)]
        t1f = carve(A2 + 129024, [128, NTOK])
        tmp6 = [carve(A2 + 135168 + i * 2048, [128, 512]) for i in range(2)]
        for kk in [("x1", i) for i in range(6, 12)] + [("tmp4", i) for i in range(2)] + [("xn5", i) for i in range(4)] \
                + [("wu", i) for i in range(2)] + [("wg", i) for i in range(2)] + ["wd"]:
            S.seed(kk, bt4)

        wg_v = w_gate.rearrange("(k p) n -> p k n", p=128)
        wu_v = w_up.rearrange("(k p) n -> p k n", p=128)
        wd_v = w_down.rearrange("(c p) n -> p c n", p=128)

        def load_gu(cg, which):
            ncol = 512 if cg < 5 else 256
            slot = cg % 2
            ring, src, nm = (wg_r, wg_v, "wg") if which == 0 else (wu_r, wu_v, "wu")
            S.dma("pool", [lambda h, k=k: h.dma_start(out=ring[slot][:, k, 0:ncol], in_=src[:, k, cg * 512: cg * 512 + ncol])
                           for k in range(8)], "%s%d" % (nm, slot), writes=[(nm, slot)])

        def load_wd(gi):
            f0 = gi * 8
            nf = min(8, NFC - f0)
            S.dma("pool", [lambda h, fl=fl: h.dma_start(out=wd[:, fl, :], in_=wd_v[:, f0 + fl, :]) for fl in range(nf)], "wd", writes=["wd"])

        load_gu(0, 0)
        load_gu(0, 1)
        load_gu(1, 0)
        load_gu(1, 1)
        load_wd(0)

        def phase4(b):
                cnd = 0 if b < 2 else 1
                for tt in range(4):
                    i = b * 4 + tt
                    if i >= 6:
                        S.dma("sp", [lambda h, i=i: h.dma_start(out=x1[i], in_=xin[i * 128:(i + 1) * 128, :])], "x1_%d" % i, writes=[("x1", i)])
                    for n in range(2):
                        pb = (2 * i + n) % 4
                        r = n
                        def mm(h, i=i, n=n, pb=pb):
                            ins = None
                            for k in range(8):
                                ins = h.matmul(bank(pb), mixT[:, k, i * 128:(i + 1) * 128], wout[:, k, n * 512:(n + 1) * 512],
                                               start=(k == 0), stop=(k == 7))
                            return ins
                        S.op("pe", mm, reads=mix_keys(b) + ["wout"], writes=[PB(pb)])
                        S.op("dve", lambda h, n=n, pb=pb, r=r, cnd=cnd: h.tensor_tensor(out=tmp4[r], in0=bank(pb), in1=gate_t[cnd][:, n * 512:(n + 1) * 512],
                                                                                 op=ALU.mult),
                             reads=[PB(pb), ("gate", cnd, n)], writes=[("tmp4", r)])
                        S.op("dve", lambda h, i=i, n=n, r=r: h.tensor_tensor(out=x1[i][:, n * 512:(n + 1) * 512], in0=x1[i][:, n * 512:(n + 1) * 512],
                                                                           in1=tmp4[r], op=ALU.add),
                             reads=[("x1", i), ("tmp4", r)], writes=[("x1", i)])

        def phase5(b):
                tiles = []
                for tt in range(4):
                    i = b * 4 + tt
                    tiles.append((x1[i], ("x1", i), xn5[tt], [("xn5", tt)], ("xn5", tt)))
                norm_block(b, tiles, G2, "G2", 3, "h2T", [4, 5], hT)

        phase4(0)
        flush_attention()
        phase4(1)
        phase5(0)
        phase4(2)
        phase5(1)
        seqs = [(0, 1024), (1024, 1280), (1280, 1536)]
        dcount = [0]

        def h2_keys(b):
            return mix_keys(b)

        def ffn_A(f, do_gate=(0, 1, 2), do_up=(0, 1, 2)):
            cg, fo = f // 4, (f % 4) * 128
            slot = cg % 2
            for b in do_gate:
                def mm(h, b=b):
                    ins = None
                    for k in range(8):
                        ins = h.matmul(bank(b), wg_r[slot][:, k, fo:fo + 128], hT[:, k, b * 512:(b + 1) * 512], start=(k == 0), stop=(k == 7))
                    return ins
                S.op("pe", mm, reads=[("wg", slot)] + h2_keys(b), writes=[PB(b)])
            for b in do_up:
                def mm(h, b=b):
                    ins = None
                    for k in range(8):
                        ins = h.matmul(bank(3 + b), wu_r[slot][:, k, fo:fo + 128], hT[:, k, b * 512:(b + 1) * 512], start=(k == 0), stop=(k == 7))
                    return ins
                S.op("pe", mm, reads=[("wu", slot)] + h2_keys(b), writes=[PB(3 + b)])

        def ffn_B(f):
            fl = f % 8
            w0 = vec[:, V_CW + f:V_CW + f + 1]
            w1 = vec[:, V_CW + 22 + f:V_CW + 22 + f + 1]
            w2 = vec[:, V_CW + 44 + f:V_CW + 44 + f + 1]
            cb = vec[:, V_CB + f:V_CB + f + 1]
            for b in range(3):
                S.op("act", lambda h, b=b: h.activation(out=gsb[:, b * 512:(b + 1) * 512], in_=bank(b), func=AF.Copy),
                     reads=[PB(b)], writes=[("gsb", b)])
            S.op("act", lambda h: h.activation(out=t1f, in_=gsb, func=AF.Identity, scale=w1, bias=cb),
                 reads=[("gsb", b) for b in range(3)] + ["vec"], writes=["t1f"])
            for (a, e) in seqs:
                S.op("dve", lambda h, a=a, e=e: h.scalar_tensor_tensor(out=t1f[:, a + 1:e], in0=gsb[:, a:e - 1], scalar=w0, in1=t1f[:, a + 1:e],
                                                                   op0=ALU.mult, op1=ALU.add),
                     reads=[("gsb", b) for b in range(3)] + ["t1f", "vec"], writes=["t1f"])
                S.op("dve", lambda h, a=a, e=e: h.scalar_tensor_tensor(out=t1f[:, a:e - 1], in0=gsb[:, a + 1:e], scalar=w2, in1=t1f[:, a:e - 1],
                                                                   op0=ALU.mult, op1=ALU.add),
                     reads=[("gsb", b) for b in range(3)] + ["t1f", "vec"], writes=["t1f"])
            S.op("act", lambda h: h.activation(out=t1f, in_=t1f, func=AF.Silu), reads=["t1f"], writes=["t1f"])

        def ffn_B2(f):
            fl = f % 8
            for b in range(3):
                S.op("dve", lambda h, b=b: h.tensor_tensor(out=uT[:, fl, b * 512:(b + 1) * 512], in0=t1f[:, b * 512:(b + 1) * 512], in1=bank(3 + b),
                                                          op=ALU.mult),
                     reads=["t1f", PB(3 + b)], writes=[("uT", fl, b)])

        def ffn_down_pre(gi, npre):
            f0 = gi * 8
            nf = min(8, NFC - f0)
            pre_banks = [6, 7, 0, 1, 2]
            for o_ in range(npre):
                i, n = o_ // 2, o_ % 2
                b = i // 4
                pb = pre_banks[o_]
                def mm(h, i=i, n=n, pb=pb):
                    ins = None
                    for fl in range(nf - 1):
                        ins = h.matmul(bank(pb), uT[:, fl, i * 128:(i + 1) * 128], wd[:, fl, n * 512:(n + 1) * 512],
                                       start=(fl == 0), stop=False)
                    return ins
                S.op("pe", mm, reads=[("uT", fl, b) for fl in range(nf - 1)] + ["wd"], writes=[PB(pb)])
                pre_done[(i, n)] = pb

        pre_done = {}

        def ffn_down(gi):
            f0 = gi * 8
            nf = min(8, NFC - f0)
            lastg = gi == 2
            for i in range(12):
                b = i // 4
                cnd = 0 if b < 2 else 1
                for n in range(2):
                    r = dcount[0] % 2
                    if lastg and (i, n) in pre_done:
                        pb = pre_done[(i, n)]
                        fls = [nf - 1]
                    else:
                        pb = 6 + dcount[0] % 2
                        fls = list(range(nf))
                    dcount[0] += 1
                    def mm(h, i=i, n=n, pb=pb, fls=fls):
                        ins = None
                        for fl in fls:
                            ins = h.matmul(bank(pb), uT[:, fl, i * 128:(i + 1) * 128], wd[:, fl, n * 512:(n + 1) * 512],
                                           start=(fl == 0), stop=(fl == nf - 1))
                        return ins
                    S.op("pe", mm, reads=[("uT", fl, b) for fl in fls] + ["wd"], writes=[PB(pb)])
                    S.op("dve", lambda h, n=n, pb=pb, r=r, cnd=cnd: h.tensor_tensor(out=tmp6[r], in0=bank(pb), in1=gate_t[cnd][:, n * 512:(n + 1) * 512],
                                                                             op=ALU.mult),
                         reads=[PB(pb), ("gate", cnd, n)], writes=[("tmp6", r)])
                    S.op("dve", lambda h, i=i, n=n, r=r: h.tensor_tensor(out=x1[i][:, n * 512:(n + 1) * 512], in0=x1[i][:, n * 512:(n + 1) * 512],
                                                                       in1=tmp6[r], op=ALU.add),
                         reads=[("x1", i), ("tmp6", r)], writes=[("x1", i)])
                if lastg:
                    S.dma("sp", [lambda h, i=i: h.dma_start(out=yout[i * 128:(i + 1) * 128, :], in_=x1[i])], "y%d" % i,
                          reads=[("x1", i)], is_output=True)

        ffn_A(0, do_gate=(0, 1), do_up=())
        phase5(2)
        wtoks = S.retire(["wout"])
        S.seed(("wa", 0), wtoks)
        S.seed(("wa", 1), wtoks)
        ada_bc(5, [6, 7])
        bt6 = S.all_tokens()
        for kk in [("uT", fl, b) for fl in range(8) for b in range(3)] + [("gsb", b) for b in range(3)] + ["t1f"] + [("tmp6", i) for i in range(2)]:
            S.seed(kk, bt6)
        for f in range(NFC):
            cg = f // 4
            if f == 0:
                ffn_A(0, do_gate=(2,), do_up=(0, 1, 2))
            else:
                ffn_A(f)
            ffn_B(f)
            if f % 8 == 0 and f > 0:
                gi = f // 8 - 1
                ffn_down(gi)
                load_wd(gi + 1)
            if f == NFC - 1:
                ffn_down_pre(2, 5)
            ffn_B2(f)
            if f % 4 == 3 and cg + 2 <= 5:
                load_gu(cg + 2, 0)
                load_gu(cg + 2, 1)
        ffn_down(2)

        S.finish()
    return nc


def _consts():
    bf = ml_dtypes.bfloat16
    ident = np.eye(128, dtype=np.float32)
    blk = np.zeros((128, 128), np.float32)
    blk[:64, :64] = 1.0 / 64
    blk[64:, 64:] = 1.0 / 64
    ones = np.ones((128, 128), np.float32)
    mean = np.full((128, 128), 1.0 / 128, np.float32)
    p = np.arange(128)
    d = p % 64
    j = d % 32
    partner = p - j + (j + 16) % 32
    perm = np.zeros((128, 128), np.float32)
    perm[partner, p] = 1.0
    cbf = np.stack([blk, ones, mean, perm], 1).astype(bf)
    t = np.arange(1024)
    row = (t // 64).astype(np.float64)
    col = (t % 64).astype(np.float64)
    inv = 10000.0 ** (-np.arange(16) / 16.0)
    half = d // 32
    f = j % 16
    mem = j // 16
    pos = np.where(half[:, None] == 0, row[None, :], col[None, :])
    ang = pos * inv[f][:, None]
    cosT = np.cos(ang)
    sinT = np.sin(ang) * np.where(mem == 0, -1.0, 1.0)[:, None]
    rope = np.stack([cosT, sinT], 1).astype(np.float32)
    dd = np.arange(64)
    a = 2 * np.pi * np.outer(dd, dd) / 64
    C = np.zeros((128, 128)); Sn = np.zeros((128, 128))
    for hh in range(2):
        C[hh * 64:(hh + 1) * 64, hh * 64:(hh + 1) * 64] = np.cos(a)
        Sn[hh * 64:(hh + 1) * 64, hh * 64:(hh + 1) * 64] = np.sin(a)
    dftc = np.concatenate([C, Sn], 1).astype(bf)

    def posdft(n):
        tt = np.arange(n)
        aa = 2 * np.pi * ((np.outer(tt, tt)) % n) / n
        sc = 1.0 / np.sqrt(n * 64.0)
        m = np.concatenate([np.cos(aa) * sc, -np.sin(aa) * sc], 1)
        return m.reshape(n // 128, 128, 2 * n).transpose(1, 0, 2).astype(bf)

    def posdft_half(n):
        tt = np.arange(n)
        aa = 2 * np.pi * ((np.outer(tt, tt[:n // 2])) % n) / n
        sc = 1.0 / np.sqrt(n * 64.0)
        m = np.concatenate([np.cos(aa) * sc, -np.sin(aa) * sc], 1)
        return m.reshape(n // 128, 128, n).transpose(1, 0, 2).astype(bf)

    return dict(c_ident=ident, c_bf=cbf, c_rope=rope, c_dftc=dftc, c_dfts=posdft_half(1024), c_dftp=posdft(256))


_CACHE = {}


def _in_maps(inp):
    f = lambda a: np.ascontiguousarray(np.asarray(a, dtype=np.float32))
    cs = _consts()
    shared = dict(cs)
    for k in ["w_ada", "w_in", "w_out", "w_gate", "w_up", "w_down"]:
        shared[k] = f(inp[k][0])
    shared["b_ada"] = f(inp["b_ada"]).reshape(1, 6 * D)
    shared["lamv"] = f(np.concatenate([inp["lam_q1"][0], inp["lam_k1"][0], inp["lam_q2"][0], inp["lam_k2"][0]])).reshape(1, 256)
    shared["qkg"] = f(np.concatenate([inp["q_norm_g"][0], inp["k_norm_g"][0]])).reshape(1, 128)

    def col(v, n):
        return np.asarray(v, np.float32).reshape(n, 128).T

    maps = []
    for i in range(NCORES):
        vt = np.zeros((128, NV), np.float32)
        vt[:, V_BADA:V_BADA + 48] = col(inp["b_ada"][0], 48)
        vt[:, V_N1:V_N1 + 8] = col(inp["norm1_g"][0], 8)
        vt[:, V_N2:V_N2 + 8] = col(inp["norm2_g"][0], 8)
        vt[:, V_CS:V_CS + 8] = col(inp["c"][i], 8)
        vt[:, V_CC:V_CC + 8] = col(inp["c_ctx"], 8)
        vt[:, V_CW:V_CW + 66] = col(np.asarray(inp["conv_w"][0]).reshape(-1), 66)
        vt[:, V_CB:V_CB + 22] = col(inp["conv_b"][0], 22)
        vt[:, V_QG] = np.tile(np.asarray(inp["q_norm_g"][0], np.float32), 2)
        vt[:, V_KG] = np.tile(np.asarray(inp["k_norm_g"][0], np.float32), 2)
        vt[:, V_SG] = np.asarray(inp["subln_g"][0], np.float32)
        vt[:, V_SIGN] = np.where(np.arange(128) % 2 == 0, 1.0, -1.0)
        m = dict(shared)
        m["vecT"] = vt
        m["xin"] = f(np.concatenate([inp["x_sample"][i], inp["x_prompt"][2 * i], inp["x_prompt"][2 * i + 1]], 0))
        m["ck"] = f(np.asarray(inp["cache_k"][i, 0]).reshape(H, 512, 128))
        m["cv"] = f(inp["cache_v"][i, 0])
        maps.append(m)
    return maps


def kernel(**inp):
    if "nc" not in _CACHE:
        _CACHE["nc"] = build_nc(DEBUG)
    nc = _CACHE["nc"]
    maps = _in_maps(inp)
    res = run_bass_kernel_spmd(nc, maps, core_ids=list(range(NCORES)))
    if DEBUG:
        return res
    r = res.results
    ys = np.stack([r[i]["yout"][:1024] for i in range(NCORES)], 0)
    yp = np.concatenate([r[i]["yout"][1024:].reshape(2, 256, D) for i in range(NCORES)], 0)
    nk = np.concatenate([r[i]["nk"] for i in range(NCORES)], 0).reshape(16, 1, H, 256, 2, 64)
    nv = np.concatenate([r[i]["nv"] for i in range(NCORES)], 0).reshape(16, 1, H, 256, 128)
    return (yp.astype(np.float32), ys.astype(np.float32), nk.astype(np.float32), nv.astype(np.float32))
```

```python
import os
import numpy as np
import ml_dtypes
import concourse.bass as bass
import concourse.mybir as mybir
from concourse.bass_utils import run_bass_kernel_spmd
from contextlib import ExitStack

F32 = mybir.dt.float32
BF16 = mybir.dt.bfloat16
AF = mybir.ActivationFunctionType
ALU = mybir.AluOpType

D = 1024
H = 6
DFF = 2816
NFC = DFF // 128
EPS = 1e-6
LAM_INIT = 0.8 - 0.6 * 1.0
NCORES = 8
NTOK = 1536
NV = 172
V_BADA, V_N1, V_N2, V_CS, V_CC, V_CW, V_CB, V_QG, V_KG, V_SG, V_SIGN = 0, 48, 56, 64, 72, 80, 146, 168, 169, 170, 171

DEBUG = os.environ.get("KDEBUG", "")


class Sched:
    def __init__(self, nc, es):
        self.nc = nc
        self.es = es
        self.eng = {"pe": nc.tensor, "act": nc.scalar, "dve": nc.vector, "pool": nc.gpsimd, "sp": nc.sync}
        self.sem = {e: es.enter_context(nc.semaphore("s_" + e)) for e in self.eng}
        self.cnt = {e: 0 for e in self.eng}
        self.waited = {e: {} for e in self.eng}
        self.last_w = {}
        self.readers = {}
        self.dsem = {}
        self.out_tokens = []

    def _deps(self, eng, reads, writes):
        deps = []
        for k in reads:
            t = self.last_w.get(k)
            if t is not None:
                if not (t[0] == "E" and t[1] == eng and eng == "pe"):
                    deps.append(t)
        for k in writes:
            t = self.last_w.get(k)
            if t is not None and not (t[0] == "E" and t[1] == eng and eng == "pe"):
                deps.append(t)
            for r in self.readers.get(k, ()):
                if not (r[0] == "E" and r[1] == eng and eng == "pe"):
                    deps.append(r)
        return deps

    def _emit_waits(self, eng, deps):
        h = self.eng[eng]
        w = self.waited[eng]
        need = {}
        for t in deps:
            key = (t[0], t[1])
            if t[2] > w.get(key, 0) and t[2] > need.get(key, 0):
                need[key] = t[2]
        for key, val in need.items():
            if key[0] == "E":
                h.wait_ge(self.sem[key[1]], val)
            else:
                h.wait_ge(self.dsem[key[1]][0], val)
            w[key] = val

    def _commit(self, tok, reads, writes):
        for k in writes:
            self.last_w[k] = tok
            self.readers[k] = []
        for k in reads:
            self.readers.setdefault(k, []).append(tok)

    def op(self, eng, fn, reads=(), writes=()):
        deps = self._deps(eng, reads, writes)
        self._emit_waits(eng, deps)
        ins = fn(self.eng[eng])
        ins.then_inc(self.sem[eng], 1)
        self.cnt[eng] += 1
        tok = ("E", eng, self.cnt[eng])
        self._commit(tok, reads, writes)
        return tok

    def dma(self, queue, fns, sem, reads=(), writes=(), is_output=False):
        if sem not in self.dsem:
            self.dsem[sem] = [self.es.enter_context(self.nc.semaphore("d_" + sem)), 0]
        deps = self._deps("#dma", reads, writes)
        self._emit_waits(queue, deps)
        h = self.eng[queue]
        ent = self.dsem[sem]
        for fn in fns:
            fn(h).then_inc(ent[0], 16)
            ent[1] += 16
        tok = ("D", sem, ent[1])
        self._commit(tok, reads, writes)
        if is_output:
            self.out_tokens.append(tok)
        return tok

    def retire(self, keys):
        toks = []
        for k in keys:
            t = self.last_w.pop(k, None)
            if t is not None:
                toks.append(t)
            toks.extend(self.readers.pop(k, []))
        return toks

    def seed(self, key, toks):
        self.readers.setdefault(key, []).extend(toks)

    def all_tokens(self):
        toks = [("E", e, c) for e, c in self.cnt.items() if c > 0]
        toks += [("D", n, v[1]) for n, v in self.dsem.items() if v[1] > 0]
        return toks

    def finish(self):
        self._emit_waits("sp", self.out_tokens)


def pipeline(units, order=None):
    if not units:
        return
    ns = max(len(u) for u in units)
    for t in range(len(units) + ns - 1):
        for j in range(ns):
            u = t - j
            if 0 <= u < len(units) and j < len(units[u]):
                units[u][j]()


def build_nc(debug=""):
    nc = bass.Bass("TRN2", target_bir_lowering=False)

    def din(name, shape, dt=F32):
        return nc.dram_tensor(name, list(shape), dt, kind="ExternalInput").ap()

    def dout(name, shape, dt=F32):
        return nc.dram_tensor(name, list(shape), dt, kind="ExternalOutput").ap()

    xin = din("xin", [NTOK, D])
    ck = din("ck", [H, 512, 128])
    cv = din("cv", [H, 512, 128])
    vecT = din("vecT", [128, NV])
    lamv = din("lamv", [1, 256])
    qkg = din("qkg", [1, 128])
    b_ada = din("b_ada", [1, 6 * D])
    w_ada = din("w_ada", [D, 6 * D])
    w_in = din("w_in", [D, 2560])
    w_out = din("w_out", [D, D])
    w_gate = din("w_gate", [D, DFF])
    w_up = din("w_up", [D, DFF])
    w_down = din("w_down", [DFF, D])
    c_ident = din("c_ident", [128, 128])
    c_bf = din("c_bf", [128, 4, 128], BF16)
    c_rope = din("c_rope", [128, 2, 1024])
    c_dftc = din("c_dftc", [128, 256], BF16)
    c_dfts = din("c_dfts", [128, 8, 1024], BF16)
    c_dftp = din("c_dftp", [128, 2, 512], BF16)

    yout = dout("yout", [NTOK, D])
    nk = dout("nk", [2, H, 256, 128])
    nv = dout("nv", [2, H, 256, 128])
    dbg = {}

    es = ExitStack()
    with es:
        S = Sched(nc, es)

        def sb(name, shape, dt=F32):
            return es.enter_context(nc.sbuf_tensor(name, list(shape), dt))

        ps_all = es.enter_context(nc.psum_tensor("ps_all", [128, 4096], F32))

        def bank(i, n=512, off=0):
            return ps_all[:, i * 512 + off: i * 512 + off + n]

        def PB(i):
            return ("ps", i)

        vec = sb("vec", [128, NV])
        ident = sb("ident", [128, 128])
        cbf = sb("cbf", [128, 4, 128], BF16)
        blk64, ones_bf, mean128, perm = cbf[:, 0, :], cbf[:, 1, :], cbf[:, 2, :], cbf[:, 3, :]
        mods = sb("mods", [128, 6, 8, 2])
        G1 = sb("G1", [128, 8, 2])
        G2 = sb("G2", [128, 8, 2])
        sT = sb("sT", [128, 8, 2], BF16)
        srep = sb("srep", [128, 2, 8, 128], BF16)
        lam_t = sb("lam_t", [128, 8])
        lamb = sb("lamb", [128, 256])
        qkg_bc = sb("qkg_bc", [128, 128])
        epsc = sb("epsc", [128, 1])
        tmp16 = sb("tmp16", [128, 3, 16])
        dftc = sb("dftc", [128, 256], BF16)[:]
        dftp = sb("dftp", [128, 2, 512], BF16)[:]
        lamjunk = sb("lamjunk", [128, 128])
        gate_t = [sb("gate%d" % i, [128, D]) for i in range(2)]
        bada_bc = sb("bada_bc", [128, D])
        junk = sb("junk", [128, D], BF16)
        stat = sb("stat", [128, 12, 4])

        S.dma("sp", [lambda h: h.dma_start(out=vec[:], in_=vecT)], "vec", writes=["vec"])
        S.dma("sp", [lambda h: h.dma_start(out=ident[:], in_=c_ident)], "ident", writes=["ident"])
        S.dma("sp", [lambda h: h.dma_start(out=cbf[:], in_=c_bf)], "cbf", writes=["cbf"])
        S.dma("sp", [lambda h: h.dma_start(out=lamb[:], in_=lamv.partition_broadcast(128))], "lamb", writes=["lamb"])
        S.dma("sp", [lambda h: h.dma_start(out=qkg_bc[:], in_=qkg.partition_broadcast(128))], "qkg", writes=["qkg"])
        S.op("dve", lambda h: h.memset(epsc[:], EPS), writes=["epsc"])

        A0 = 0
        A1 = 24576
        A2 = 40960
        ARENA_BYTES = A2 + 144384
        arena = sb("arena", [128, ARENA_BYTES // 4])

        def carve(off, shape, dt=F32):
            n = 1
            for s_ in shape[1:]:
                n *= s_
            assert off % 4 == 0
            if dt == F32:
                assert off + 4 * n <= ARENA_BYTES, (off, shape)
                ap = arena[:, off // 4: off // 4 + n]
            else:
                assert n % 2 == 0 and off + 2 * n <= ARENA_BYTES, (off, shape)
                ap = arena[:, off // 4: off // 4 + n // 2].bitcast(BF16)
            if len(shape) == 3:
                ap = ap.rearrange("p (a b) -> p a b", a=shape[1])
            elif len(shape) == 4:
                ap = ap.rearrange("p (a b c) -> p a b c", a=shape[1], b=shape[2])
            return ap

        hT = carve(A0, [128, 8, NTOK], BF16)
        mixT = hT
        wa_slots = [carve(A1 + i * 8192, [128, 8, 512], BF16) for i in range(2)]
        o = A2
        V_own = carve(o, [128, 12, 768], BF16); o += 18432
        V_cache = carve(o, [128, 4, 768], BF16); o += 6144
        kT_cache = carve(o, [128, 6, 512], BF16); o += 6144
        o += 2048
        xfT = carve(o, [128, 2, NTOK], BF16); o += 6144
        qT = carve(o, [128, 6, NTOK], BF16); o += 18432
        kT_own = carve(o, [128, 6, NTOK], BF16); o += 18432
        NXT = 12
        xt = [carve(A2 + i * 4096, [128, D]) for i in range(8)] + [carve(o - 16384 + i * 4096, [128, D]) for i in range(4)]
        P23_END = o
        win_slots = [carve(o + i * 8192, [128, 8, 512], BF16) for i in range(3)]; o += 24576
        sq_r = [carve(o + i * 1024, [128, 512], BF16) for i in range(2)]; o += 2048
        qnb_r = [carve(o + i * 1024, [128, 512], BF16) for i in range(2)]; o += 2048
        ln_r = [carve(o + i * 2048, [128, 512]) for i in range(2)]; o += 4096
        qn_r = [carve(o + i * 2048, [128, 512]) for i in range(2)]; o += 4096
        t1_r = [carve(o + i * 2048, [128, 512]) for i in range(2)]; o += 4096
        ropeT = carve(o, [128, 2, 1024]); o += 8192
        kst_r = [carve(o + i * 3072, [128, 768]) for i in range(2)]; o += 6144
        vst_r = [carve(o + i * 3072, [128, 768]) for i in range(2)]; o += 6144
        ksq = carve(o, [128, 768]); o += 3072
        ckst_r = [carve(o + i * 2048, [128, 4, 128]) for i in range(2)]; o += 4096
        assert o <= ARENA_BYTES, o

        wada_v = w_ada.rearrange("(k p) n -> p k n", p=128)
        win_v = w_in.rearrange("(k p) n -> p k n", p=128)
        wa_cnt = [0]
        kst_off = A2 + 75776 + 24576 + 2048 + 2048 + 4096 + 4096 + 4096 + 8192
        wa_slots = wa_slots + [carve(kst_off + i * 8192, [128, 8, 512], BF16) for i in range(2)]

        def load_wada_half(j, hf):
            slot = wa_cnt[0] % 2 if wa_cnt[0] >= 4 else wa_cnt[0]
            wa_cnt[0] += 1
            key = ("wa", slot)
            c0 = j * 1024 + hf * 512
            S.dma("pool", [lambda h, k=k: h.dma_start(out=wa_slots[slot][:, k, :], in_=wada_v[:, k, c0:c0 + 512])
                           for k in range(8)], "wa%d" % slot, writes=[key])
            return slot

        cond = vec[:, V_CS:V_CS + 16]
        S.op("act", lambda h: h.activation(out=tmp16[:, 0, :], in_=cond, func=AF.Exp, scale=-1.0),
             reads=["vec"], writes=["t16a"])
        S.op("dve", lambda h: h.tensor_scalar(out=tmp16[:, 1, :], in0=tmp16[:, 0, :], scalar1=1.0, scalar2=None, op0=ALU.add),
             reads=["t16a"], writes=["t16b"])
        S.op("dve", lambda h: h.reciprocal(out=tmp16[:, 2, :], in_=tmp16[:, 1, :]), reads=["t16b"], writes=["t16c"])
        S.op("dve", lambda h: h.tensor_tensor(out=sT[:].rearrange("p k c -> p c k"),
                                              in0=cond.rearrange("p (c k) -> p c k", c=2),
                                              in1=tmp16[:, 2, :].rearrange("p (c k) -> p c k", c=2), op=ALU.mult),
             reads=["t16c", "vec"], writes=["sT"])
        for c in range(2):
            S.op("dve", lambda h, c=c: h.tensor_copy(out=srep[:, c, :, :], in_=sT[:, :, c:c + 1].to_broadcast([128, 8, 128])),
                 reads=["sT"], writes=[("srep", c)])
        S.op("dve", lambda h: h.tensor_tensor(out=lamjunk[:].rearrange("p (a b) -> p a b", a=2),
                                              in0=lamb[:].rearrange("p (a c b) -> p a c b", a=2, c=2)[:, :, 0, :],
                                              in1=lamb[:].rearrange("p (a c b) -> p a c b", a=2, c=2)[:, :, 1, :], op=ALU.mult),
             reads=["lamb"], writes=["lamjunk"])
        S.op("dve", lambda h: h.tensor_reduce(out=lam_t[:, 0:2], in_=lamjunk[:].rearrange("p (a b) -> p a b", a=2),
                                              axis=mybir.AxisListType.X, op=ALU.add),
             reads=["lamjunk"], writes=["lam0", "lam1"])
        S.op("act", lambda h: h.activation(out=lam_t[:, 2:4], in_=lam_t[:, 0:2], func=AF.Exp), reads=["lam0", "lam1"], writes=["lam2"])
        S.op("dve", lambda h: h.tensor_tensor(out=lam_t[:, 4:5], in0=lam_t[:, 2:3], in1=lam_t[:, 3:4], op=ALU.subtract),
             reads=["lam2"], writes=["lam4"])
        S.op("dve", lambda h: h.tensor_scalar(out=lam_t[:, 5:6], in0=lam_t[:, 4:5], scalar1=LAM_INIT, scalar2=-1.0, op0=ALU.add, op1=ALU.mult),
             reads=["lam4"], writes=["neglam"])
        S.op("dve", lambda h: h.tensor_scalar(out=lam_t[:, 6:7], in0=vec[:, V_SG:V_SG + 1], scalar1=1.0 - LAM_INIT, scalar2=None, op0=ALU.mult),
             reads=["vec"], writes=["sgs"])
        neglam = lam_t[:, 5:6]
        sgs = lam_t[:, 6:7]

        def ada_pp(j, pbank=7):
            for hf in range(2):
                slot = load_wada_half(j, hf)
                def mm(h, slot=slot, hf=hf):
                    ins = None
                    for c4 in range(4):
                        c = hf * 4 + c4
                        for k in range(8):
                            ins = h.matmul(bank(pbank, 2, 2 * c), wa_slots[slot][:, k, c4 * 128:(c4 + 1) * 128], sT[:, k, :],
                                           start=(k == 0), stop=(k == 7))
                    return ins
                S.op("pe", mm, reads=[("wa", slot), "sT"], writes=[PB(pbank)])
            S.op("dve", lambda h: h.tensor_tensor(out=mods[:, j, :, :],
                                                  in0=bank(pbank, 16).rearrange("p (c t) -> p c t", t=2),
                                                  in1=vec[:, V_BADA + j * 8:V_BADA + j * 8 + 8].unsqueeze(2).to_broadcast([128, 8, 2]),
                                                  op=ALU.add),
                 reads=[PB(pbank), "vec"], writes=[("mods", j)])

        def ada_bc(j, pbanks):
            S.dma("sp", [lambda h: h.dma_start(out=bada_bc[:], in_=b_ada[:, j * 1024:(j + 1) * 1024].partition_broadcast(128))],
                  "badabc", writes=["badabc"])
            for n in range(2):
                slot = load_wada_half(j, n)
                for c in range(2):
                    pb = pbanks[c]
                    def mm(h, c=c, slot=slot, pb=pb):
                        ins = None
                        for k in range(8):
                            ins = h.matmul(bank(pb), srep[:, c, k, :], wa_slots[slot][:, k, :], start=(k == 0), stop=(k == 7))
                        return ins
                    S.op("pe", mm, reads=[("wa", slot), ("srep", c)], writes=[PB(pb)])
                    S.op("dve", lambda h, c=c, n=n, pb=pb: h.tensor_tensor(out=gate_t[c][:, n * 512:(n + 1) * 512], in0=bank(pb),
                                                                         in1=bada_bc[:, n * 512:(n + 1) * 512], op=ALU.add),
                         reads=[PB(pb), "badabc"], writes=[("gate", c, n)])

        for i in range(8):
            pass
        def cache_k_prep():
            for hh in range(H):
                r = hh % 2
                S.dma("sp", [lambda h, hh=hh, r=r: h.dma_start(out=ckst_r[r], in_=ck[hh].rearrange("(j p) d -> p j d", p=128))],
                      "ckst%d" % r, writes=[("ckst", r)])
                pb = [6, 7][r]
                def tp(h, r=r, pb=pb):
                    ins = None
                    for j in range(4):
                        ins = h.transpose(bank(pb, 128, j * 128), ckst_r[r][:, j, :], ident[:])
                    return ins
                S.op("pe", tp, reads=[("ckst", r), "ident"], writes=[PB(pb)])
                S.op("act", lambda h, hh=hh, pb=pb: h.activation(out=kT_cache[:, hh, :], in_=bank(pb), func=AF.Copy),
                     reads=[PB(pb)], writes=["kTc", ("kTc", hh)])


        def load_x(i):
            slot = i % NXT
            S.dma("sp", [lambda h: h.dma_start(out=xt[slot], in_=xin[i * 128:(i + 1) * 128, :])], "xt%d" % slot,
                  writes=[("xt", slot)])
            return xt[slot], ("xt", slot)

        for i in range(NXT):
            load_x(i)

        def p1_tiles(b):
            tl = []
            for tt in range(4):
                i = b * 4 + tt
                kx = ("xt", i)
                tl.append((xt[i], kx, xt[i], [kx, kx + ("n",)], kx + ("n",)))
            return tl

        def norm_block(b, tiles, Gm, Gkey, shpart, tag, tp_banks, dst, ex_reads=(), do_stats=True, do_tp=True, defer=None):
            cnd = 0 if b < 2 else 1
            for tt in (range(4) if do_stats else ()):
                i = b * 4 + tt
                xa, xkey, xn, xnw, xnr = tiles[tt]
                S.op("act", lambda h, xa=xa, i=i: h.activation(out=junk[:], in_=xa, func=AF.Square, accum_out=stat[:, i, 0:1]),
                     reads=[xkey], writes=["junk", ("st0", tag, i)])
                S.op("act", lambda h, i=i: h.activation(out=stat[:, i, 1:2], in_=stat[:, i, 0:1], func=AF.Ln,
                                                       scale=1.0 / D, bias=epsc[:]),
                     reads=[("st0", tag, i), "epsc"], writes=[("st1", tag, i)])
                S.op("act", lambda h, i=i: h.activation(out=stat[:, i, 2:3], in_=stat[:, i, 1:2], func=AF.Exp, scale=-0.5),
                     reads=[("st1", tag, i)], writes=[("st2", tag, i)])
                S.op("dve", lambda h, xa=xa, xn=xn, i=i: h.tensor_scalar(out=xn, in0=xa, scalar1=stat[:, i, 2:3], scalar2=None, op0=ALU.mult),
                     reads=[xkey, ("st2", tag, i)], writes=xnw)
            def group(c):
                pb = tp_banks[c % len(tp_banks)]
                def tp(h, c=c, pb=pb):
                    ins = None
                    for tt in range(4):
                        ins = h.transpose(bank(pb, 128, tt * 128), tiles[tt][2][:, c * 128:(c + 1) * 128], ident[:])
                    return ins
                S.op("pe", tp, reads=[t[4] for t in tiles] + ["ident"], writes=[PB(pb)])
                wk = [("hT", c, b)] + ([("mixp", c, 0), ("mixp", c, 1)] if b == 2 else [])
                if c % 2 == 0:
                    S.op("dve", lambda h, c=c, pb=pb: h.tensor_scalar(
                        out=dst[:, c, b * 512:(b + 1) * 512], in0=bank(pb), scalar1=Gm[:, c, cnd:cnd + 1],
                        scalar2=mods[:, shpart, c, cnd:cnd + 1], op0=ALU.mult, op1=ALU.add),
                        reads=[PB(pb), Gkey, ("mods", shpart)] + list(ex_reads), writes=wk)
                else:
                    S.op("act", lambda h, c=c, pb=pb: h.activation(
                        out=dst[:, c, b * 512:(b + 1) * 512], in_=bank(pb), func=AF.Identity,
                        scale=Gm[:, c, cnd:cnd + 1], bias=mods[:, shpart, c, cnd:cnd + 1]),
                        reads=[PB(pb), Gkey, ("mods", shpart)] + list(ex_reads), writes=wk)
            for c in (range(8) if do_tp else ()):
                if defer is None:
                    group(c)
                else:
                    defer.append(lambda c=c: group(c))

        for b in range(3):
            norm_block(b, p1_tiles(b), G1, "G1", 0, "hT", [0, 1, 2, 3], hT, do_tp=False)
        ada_pp(0)
        ada_pp(1)
        wa_extra = S.retire([("wa", 2), ("wa", 3)])
        for kk in [("kst", 0), ("kst", 1), ("vst", 0, 0), ("vst", 0, 1), ("vst", 1, 0), ("vst", 1, 1), "ksq0", "ksq1", ("ckst", 0), ("ckst", 1)]:
            S.seed(kk, wa_extra)
        S.op("dve", lambda h: h.scalar_tensor_tensor(out=G1[:], in0=mods[:, 1, :, :], scalar=1.0,
                                                     in1=vec[:, V_N1:V_N1 + 8].unsqueeze(2).to_broadcast([128, 8, 2]),
                                                     op0=ALU.add, op1=ALU.mult),
             reads=[("mods", 1), "vec"], writes=["G1"])
        for b in range(3):
            norm_block(b, p1_tiles(b), G1, "G1", 0, "hT", [0, 1, 2, 3], hT, do_stats=False)
        xt_free = S.retire([("xt", s_) for s_ in range(NXT)] + [("xt", s_, "n") for s_ in range(NXT)])
        for i in range(12):
            S.seed(("V", i), xt_free)
            S.seed(("V", i, 1), xt_free)
        S.seed(("Vc", 0), xt_free)
        S.seed("kTc", xt_free)
        for hh_ in range(H):
            for b_ in range(3):
                S.seed(("kT", hh_, b_), xt_free)

        if debug == "h":
            dbg["hT"] = dout("dbg_hT", [128, 8, NTOK], BF16)
            S.dma("sp", [lambda h: h.dma_start(out=dbg["hT"], in_=hT)], "dbg",
                  reads=[("hT", c, b) for c in range(8) for b in range(3)], is_output=True)
            S.finish()
            return nc

        S.dma("sp", [lambda h: h.dma_start(out=ropeT, in_=c_rope)], "rope", writes=["rope"])
        S.dma("sp", [lambda h: h.dma_start(out=dftc, in_=c_dftc)], "dftc", writes=["dftc"])
        S.dma("sp", [lambda h: h.dma_start(out=dftp, in_=c_dftp)], "dftp", writes=["dftp"])
        cosT = ropeT[:, 0, :]
        sinT = ropeT[:, 1, :]

        def load_win(g):
            slot = g % 3
            S.dma("pool", [lambda h, k=k: h.dma_start(out=win_slots[slot][:, k, :], in_=win_v[:, k, g * 512:(g + 1) * 512])
                           for k in range(8)], "win%d" % slot, writes=[("win", slot)])

        load_win(0)
        load_win(1)
        load_win(2)

        def hT_keys(b):
            return [("hT", c, b) for c in range(8)]

        ucount = [0]

        def qk_unit(m, b):
            u = ucount[0]
            ucount[0] += 1
            g, mc = m // 4, m % 4
            slot = g % 3
            pj = [0, 1, 2][u % 3]
            st = {}

            def proj():
                def mm(h):
                    ins = None
                    for k in range(8):
                        ins = h.matmul(bank(pj), win_slots[slot][:, k, mc * 128:(mc + 1) * 128], hT[:, k, b * 512:(b + 1) * 512],
                                       start=(k == 0), stop=(k == 7))
                    return ins
                S.op("pe", mm, reads=[("win", slot)] + hT_keys(b), writes=[PB(pj)])
            st["proj"] = proj
            if m < 2:
                def cp():
                    S.op("act", lambda h: h.activation(out=xfT[:, m, b * 512:(b + 1) * 512], in_=bank(pj), func=AF.Copy),
                         reads=[PB(pj)], writes=[("xfT", m, b)])
                st["sq"] = cp
                return st
            isq = m < 8
            hh = (m - 2) if isq else (m - 8)
            gcol = vec[:, V_QG:V_QG + 1] if isq else vec[:, V_KG:V_KG + 1]
            dstT = qT if isq else kT_own
            dkey = ("qT" if isq else "kT", hh, b)
            r = u % 2
            msb = [3, 4][u % 2]
            rtb = 5
            sq, lnb, qn, t1, qnb = sq_r[r], ln_r[r], qn_r[r], t1_r[r], qnb_r[r]

            def s_sq():
                S.op("act", lambda h: h.activation(out=sq, in_=bank(pj), func=AF.Square), reads=[PB(pj)], writes=[("sq", r)])
            def s_ms():
                S.op("pe", lambda h: h.matmul(bank(msb), blk64, sq, start=True, stop=True), reads=[("sq", r), "cbf"], writes=[PB(msb)])
            def s_ln():
                S.op("act", lambda h: h.activation(out=lnb, in_=bank(msb), func=AF.Ln, bias=epsc[:]),
                     reads=[PB(msb), "epsc"], writes=[("ln", r)])
            def s_exp():
                S.op("act", lambda h: h.activation(out=lnb, in_=lnb, func=AF.Exp, scale=-0.5), reads=[("ln", r)], writes=[("ln", r)])
            if b == 2:
                def s_stt():
                    S.op("dve", lambda h: h.scalar_tensor_tensor(out=dstT[:, hh, b * 512:(b + 1) * 512], in0=bank(pj), scalar=gcol, in1=lnb,
                                                                 op0=ALU.mult, op1=ALU.mult),
                         reads=[PB(pj), ("ln", r), "vec"], writes=[dkey])
                st["sq"] = s_sq
                st["c"] = lambda: [f() for f in (s_ms, s_ln, s_exp, s_stt)]
                return st
            def s_stt():
                S.op("dve", lambda h: h.scalar_tensor_tensor(out=qn, in0=bank(pj), scalar=gcol, in1=lnb, op0=ALU.mult, op1=ALU.mult),
                     reads=[PB(pj), ("ln", r), "vec"], writes=[("qn", r)])
            def s_cast():
                S.op("act", lambda h: h.activation(out=qnb, in_=qn, func=AF.Copy), reads=[("qn", r)], writes=[("qnb", r)])
            def s_rot():
                S.op("pe", lambda h: h.matmul(bank(rtb), perm, qnb, start=True, stop=True), reads=[("qnb", r), "cbf"], writes=[PB(rtb)])
            def s_t1():
                S.op("dve", lambda h: h.tensor_tensor(out=t1, in0=qn, in1=cosT[:, b * 512:(b + 1) * 512], op=ALU.mult),
                     reads=[("qn", r), "rope"], writes=[("t1", r)])
            def s_t2():
                S.op("dve", lambda h: h.tensor_tensor(out=qn, in0=bank(rtb), in1=sinT[:, b * 512:(b + 1) * 512], op=ALU.mult),
                     reads=[PB(rtb), "rope"], writes=[("qn", r)])
            def s_add():
                S.op("dve", lambda h: h.tensor_tensor(out=dstT[:, hh, b * 512:(b + 1) * 512], in0=t1, in1=qn, op=ALU.add),
                     reads=[("t1", r), ("qn", r)], writes=[dkey])
            st["sq"] = s_sq
            st["c"] = lambda: [f() for f in (s_ms, s_ln, s_exp, s_stt, s_t1)]
            st["cast"] = s_cast
            st["d"] = lambda: [f() for f in (s_rot, s_t2, s_add)]
            return st

        def pipeline_qk(units):
            n = len(units)
            def call(i, nm):
                if 0 <= i < n and nm in units[i]:
                    units[i][nm]()
            for t in range(n + 4):
                call(t - 4, "d")
                call(t - 2, "c")
                call(t - 1, "sq")
                call(t - 2, "cast")
                call(t, "proj")

        kstat = sb("kstat", [128, 2, 12])
        def kdup_tile(ti, B6, B7):
            i = 8 + ti
            r = ti % 2
            kst = kst_r[r]
            def mm(h, i=i, B6=B6, B7=B7):
                ins = None
                for k in range(8):
                    ins = h.matmul(bank(B6), hT[:, k, i * 128:(i + 1) * 128], win_slots[2][:, k, :], start=(k == 0), stop=(k == 7))
                for k in range(8):
                    ins = h.matmul(bank(B7, 256), hT[:, k, i * 128:(i + 1) * 128], win_slots[0][:, k, 0:256], start=(k == 0), stop=(k == 7))
                return ins
            S.op("pe", mm, reads=[("win", 2), ("win", 0)] + hT_keys(2), writes=[PB(B6), PB(B7)])
            S.op("act", lambda h, B6=B6: h.activation(out=ksq[:, 0:512], in_=bank(B6), func=AF.Square), reads=[PB(B6)], writes=["ksq0"])
            S.op("act", lambda h, B7=B7: h.activation(out=ksq[:, 512:768], in_=bank(B7, 256), func=AF.Square), reads=[PB(B7)], writes=["ksq1"])
            S.op("dve", lambda h: h.tensor_reduce(out=kstat[:, 0, :], in_=ksq.rearrange("p (g d) -> p g d", d=64),
                                                  axis=mybir.AxisListType.X, op=ALU.add),
                 reads=["ksq0", "ksq1"], writes=["kstat0"])
            S.op("act", lambda h: h.activation(out=kstat[:, 1, :], in_=kstat[:, 0, :], func=AF.Ln, scale=1.0 / 64, bias=epsc[:]),
                 reads=["kstat0", "epsc"], writes=["kstat1"])
            S.op("act", lambda h: h.activation(out=kstat[:, 1, :], in_=kstat[:, 1, :], func=AF.Exp, scale=-0.5),
                 reads=["kstat1"], writes=["kstat1"])
            S.op("dve", lambda h, kst=kst, B6=B6: h.tensor_tensor(out=kst[:, 0:512].rearrange("p (g d) -> p g d", d=64),
                                                           in0=bank(B6).rearrange("p (g d) -> p g d", d=64),
                                                           in1=kstat[:, 1, 0:8].unsqueeze(2).to_broadcast([128, 8, 64]), op=ALU.mult),
                 reads=[PB(B6), "kstat1"], writes=[("kst", r)])
            S.op("dve", lambda h, kst=kst, B7=B7: h.tensor_tensor(out=kst[:, 512:768].rearrange("p (g d) -> p g d", d=64),
                                                           in0=bank(B7, 256).rearrange("p (g d) -> p g d", d=64),
                                                           in1=kstat[:, 1, 8:12].unsqueeze(2).to_broadcast([128, 4, 64]), op=ALU.mult),
                 reads=[PB(B7), "kstat1"], writes=[("kst", r)])
            S.op("dve", lambda h, kst=kst: h.tensor_tensor(out=kst.rearrange("p (g d) -> p g d", d=64),
                                                           in0=kst.rearrange("p (g d) -> p g d", d=64),
                                                           in1=qkg_bc[:, 64:128].unsqueeze(1).to_broadcast([128, 12, 64]), op=ALU.mult),
                 reads=[("kst", r), "qkg"], writes=[("kst", r)])
            pi, tt = ti // 2, ti % 2
            S.dma("sp", [lambda h, kst=kst, pi=pi, tt=tt: h.dma_start(
                out=nk[pi, :, tt * 128:(tt + 1) * 128, :].rearrange("h s d -> s h d"),
                in_=kst.rearrange("p (h d) -> p h d", d=128))], "kst%d" % r,
                reads=[("kst", r)], is_output=True)

        def v_tile(i, B6, B7):
            b = i // 4
            def mm(h, i=i, B6=B6, B7=B7):
                ins = None
                for k in range(8):
                    ins = h.matmul(bank(B6, 256), hT[:, k, i * 128:(i + 1) * 128], win_slots[0][:, k, 256:512], start=(k == 0), stop=(k == 7))
                for k in range(8):
                    ins = h.matmul(bank(B7), hT[:, k, i * 128:(i + 1) * 128], win_slots[1][:, k, :], start=(k == 0), stop=(k == 7))
                return ins
            S.op("pe", mm, reads=[("win", 0), ("win", 1)] + hT_keys(b), writes=[PB(B6), PB(B7)])
            if i < 8:
                S.op("act", lambda h, i=i, B6=B6: h.activation(out=V_own[:, i, 0:256], in_=bank(B6, 256), func=AF.Copy),
                     reads=[PB(B6)], writes=[("V", i)])
                S.op("dve", lambda h, i=i, B7=B7: h.tensor_copy(out=V_own[:, i, 256:768], in_=bank(B7)),
                     reads=[PB(B7)], writes=[("V", i, 1)])
            else:
                r = i % 2
                vst = vst_r[r]
                S.op("act", lambda h, vst=vst, B6=B6: h.activation(out=vst[:, 0:256], in_=bank(B6, 256), func=AF.Copy),
                     reads=[PB(B6)], writes=[("vst", r, 0)])
                S.op("dve", lambda h, vst=vst, B7=B7: h.tensor_copy(out=vst[:, 256:768], in_=bank(B7)),
                     reads=[PB(B7)], writes=[("vst", r, 1)])
                S.op("dve", lambda h, vst=vst, i=i: h.tensor_copy(out=V_own[:, i, :], in_=vst),
                     reads=[("vst", r, 0), ("vst", r, 1)], writes=[("V", i), ("V", i, 1)])
                pi, tt = (i - 8) // 2, (i - 8) % 2
                S.dma("sp", [lambda h, vst=vst, pi=pi, tt=tt: h.dma_start(
                    out=nv[pi, :, tt * 128:(tt + 1) * 128, :].rearrange("h s d -> s h d"),
                    in_=vst.rearrange("p (h d) -> p h d", d=128))], "vst%d" % r,
                    reads=[("vst", r, 0), ("vst", r, 1)], is_output=True)

        units = []
        for m in range(0, 14):
            for b in range(3):
                units.append(qk_unit(m, b))
            if m == 3:
                units.append({"proj": lambda: ada_bc(2, [6, 7])})
            if m == 5:
                units.append({"proj": lambda: load_win(3)})
            if m == 6:
                units.append({"proj": lambda: ada_pp(3, 6)})
            if m == 9:
                units.append({"proj": lambda: ada_pp(4, 7)})
                units.append({"proj": lambda: load_win(4)})
            if m == 11:
                units.append({"proj": cache_k_prep})
        for i_ in (8, 9, 10, 11):
            units.append({"proj": lambda i_=i_: v_tile(i_, 6, 7)})
        pipeline_qk(units)
        dfts = carve(A1, [128, 8, 1024], BF16)
        S.seed("dfts", S.retire([("wa", 0), ("wa", 1)]))
        S.dma("sp", [lambda h, k=k: h.dma_start(out=dfts[:, k, :], in_=c_dfts[:, k, :]) for k in range(8)], "dfts", writes=["dfts"])

        tm_pairs = [(6, 7), (3, 4), (0, 1), (2, 5)]
        tm_seq = [("k", 0), ("v", 0), ("v", 1), ("k", 1), ("v", 2), ("k", 2), ("v", 3), ("v", 4), ("k", 3),
                  ("v", 5), ("v", 6), ("v", 7)]
        for n_, (kind_, idx_) in enumerate(tm_seq):
            B6_, B7_ = tm_pairs[n_ % 4]
            if kind_ == "k":
                kdup_tile(idx_, B6_, B7_)
            else:
                v_tile(idx_, B6_, B7_)

        S.dma("pool", [lambda h, hh=hh: h.dma_start(out=V_cache[:, :, hh * 128:(hh + 1) * 128],
                                                    in_=cv[hh].rearrange("(j p) d -> p j d", p=128)) for hh in range(H)],
              "vcache", writes=[("Vc", 0)])
        if debug == "qkv":
            for nm, t_, shp, keys in [("qT", qT, [128, 6, NTOK], [("qT", hh, b) for hh in range(6) for b in range(3)]),
                                      ("kT", kT_own, [128, 6, NTOK], [("kT", hh, b) for hh in range(6) for b in range(3)]),
                                      ("xfT", xfT, [128, 2, NTOK], [("xfT", m, b) for m in range(2) for b in range(3)]),
                                      ("V", V_own, [128, 12, 768], [("V", i) for i in range(12)] + [("V", i, 1) for i in range(12)]),
                                      ("Vc", V_cache, [128, 4, 768], [("Vc", 0)]),
                                      ("kTc", kT_cache, [128, 6, 512], [("kTc", hh) for hh in range(6)])]:
                dbg[nm] = dout("dbg_" + nm, shp, BF16)
                S.dma("sp", [lambda h, nm=nm, t_=t_: h.dma_start(out=dbg[nm], in_=t_)], "dbg_" + nm, reads=keys, is_output=True)
            S.finish()
            return nc

        bt = S.all_tokens()
        o3 = P23_END
        o3 += 32768
        o3 += 2048 + 512
        u_tok = carve(o3, [128, 12, 512], BF16); o3 += 12288
        E_r = [carve(o3 + i * 2048, [128, 2, 512], BF16) for i in range(3)]; o3 += 6144
        zc = [carve(o3 + i * 2048, [128, 512]) for i in range(2)]; o3 += 4096
        oc = [carve(o3 + i * 2048, [128, 512]) for i in range(2)]; o3 += 4096
        of = carve(o3, [128, 512]); o3 += 2048
        osq = carve(o3, [128, 512], BF16); o3 += 1024
        oln = carve(o3, [128, 512]); o3 += 2048
        assert o3 <= ARENA_BYTES
        for kk in ["of", "osq", "oln", "ualt"] + [("utok", i) for i in range(12)] + [("E", i) for i in range(3)] \
                + [("zc", i) for i in range(2)] + [("oc", i) for i in range(2)]:
            S.seed(kk, bt)

        S.op("dve", lambda h: h.scalar_tensor_tensor(out=G2[:], in0=mods[:, 4, :, :], scalar=1.0,
                                                     in1=vec[:, V_N2:V_N2 + 8].unsqueeze(2).to_broadcast([128, 8, 2]),
                                                     op0=ALU.add, op1=ALU.mult),
             reads=[("mods", 4), "vec"], writes=["G2"])

        for i in range(12):
            pb = i % 4
            b = i // 4
            def mm(h, i=i, pb=pb):
                ins = None
                for fc in range(2):
                    ins = h.matmul(bank(pb, 256, fc * 256), xfT[:, fc, i * 128:(i + 1) * 128], dftc, start=True, stop=True)
                return ins
            S.op("pe", mm, reads=[("xfT", 0, b), ("xfT", 1, b), "dftc"], writes=[PB(pb)])
            if i % 2 == 0:
                S.op("act", lambda h, i=i, pb=pb: h.activation(out=u_tok[:, i, :], in_=bank(pb), func=AF.Copy),
                     reads=[PB(pb)], writes=[("utok", i)])
            else:
                S.op("dve", lambda h, i=i, pb=pb: h.tensor_copy(out=u_tok[:, i, :], in_=bank(pb)),
                     reads=[PB(pb)], writes=[("utok", i)])
        fcount = [0]
        def fourier2(tiles, table, tkey, ns, tok0, ncols_blocks, wkeys_fn):
            for fc in range(2):
                for tb, (c0, ncol) in enumerate(ncols_blocks):
                    pb = fcount[0] % 4
                    fcount[0] += 1
                    def mm(h, fc=fc, c0=c0, ncol=ncol, pb=pb):
                        ins = None
                        n = len(tiles) * 2
                        idx = 0
                        for jl, j in enumerate(tiles):
                            for cs in range(2):
                                ins = h.matmul(bank(pb, ncol), u_tok[:, j, fc * 256 + cs * 128: fc * 256 + cs * 128 + 128],
                                               table[:, jl, cs * ns + c0: cs * ns + c0 + ncol], start=(idx == 0), stop=(idx == n - 1))
                                idx += 1
                        return ins
                    S.op("pe", mm, reads=[("utok", j) for j in tiles] + [tkey], writes=[PB(pb)])
                    wk = wkeys_fn(fc, tb)
                    if fcount[0] % 2 == 0:
                        S.op("act", lambda h, fc=fc, c0=c0, ncol=ncol, pb=pb: h.activation(
                            out=mixT[:, fc, tok0 + c0: tok0 + c0 + ncol], in_=bank(pb, ncol), func=AF.Copy), reads=[PB(pb)], writes=wk)
                    else:
                        S.op("dve", lambda h, fc=fc, c0=c0, ncol=ncol, pb=pb: h.tensor_copy(
                            out=mixT[:, fc, tok0 + c0: tok0 + c0 + ncol], in_=bank(pb, ncol)), reads=[PB(pb)], writes=wk)
        fourier2([8, 9], dftp, "dftp", 256, 1024, [(0, 256)], lambda fc, tb: [("hT", fc, 2), ("mixp", fc, 0)])
        fourier2([10, 11], dftp, "dftp", 256, 1280, [(0, 256)], lambda fc, tb: [("hT", fc, 2), ("mixp", fc, 1)])
        u_alt = carve(P23_END + 22528, [128, 8, 512], BF16)
        S.op("dve", lambda h: h.tensor_scalar(out=u_alt.rearrange("p a b -> p (a b)"), in0=u_tok[:, 0:8, :].rearrange("p a b -> p (a b)"),
                                              scalar1=vec[:, V_SIGN:V_SIGN + 1], scalar2=None, op0=ALU.mult),
             reads=[("utok", i) for i in range(8)] + ["vec"], writes=["ualt"])
        for fc in range(2):
            for tb in range(2):
                pb = fcount[0] % 4
                fcount[0] += 1
                usrc = u_tok if tb == 0 else u_alt
                def mm(h, fc=fc, pb=pb, usrc=usrc):
                    ins = None
                    idx = 0
                    for j in range(8):
                        for cs in range(2):
                            ins = h.matmul(bank(pb), usrc[:, j, fc * 256 + cs * 128: fc * 256 + cs * 128 + 128],
                                           dfts[:, j, cs * 512: cs * 512 + 512], start=(idx == 0), stop=(idx == 15))
                            idx += 1
                    return ins
                S.op("pe", mm, reads=[("utok", j) for j in range(8)] + ["dfts", "ualt"], writes=[PB(pb)])
                if fcount[0] % 2 == 0:
                    S.op("act", lambda h, fc=fc, tb=tb, pb=pb: h.activation(out=mixT[:, fc, tb * 512:(tb + 1) * 512], in_=bank(pb), func=AF.Copy),
                         reads=[PB(pb)], writes=[("hT", fc, tb)])
                else:
                    S.op("dve", lambda h, fc=fc, tb=tb, pb=pb: h.tensor_copy(out=mixT[:, fc, tb * 512:(tb + 1) * 512], in_=bank(pb)),
                         reads=[PB(pb)], writes=[("hT", fc, tb)])
        wout = carve(A1, [128, 8, D], BF16)
        wout_v = w_out.rearrange("(k p) n -> p k n", p=128)
        S.seed("wout", S.retire(["dfts"]))
        S.dma("pool", [lambda h, k=k: h.dma_start(out=wout[:, k, :], in_=wout_v[:, k, :]) for k in range(8)], "wout", writes=["wout"])

        x1 = [carve(A2 + 98304 + i * 4096, [128, D]) for i in range(6)] + [carve(A2 + i * 4096, [128, D]) for i in range(6)]
        assert A2 + 98304 >= P23_END + 19968 and A2 + 98304 + 6 * 4096 <= P23_END + 47616
        btf = S.all_tokens()
        for i in range(6):
            S.seed(("x1", i), btf)
            S.dma("sp", [lambda h, i=i: h.dma_start(out=x1[i], in_=xin[i * 128:(i + 1) * 128, :])], "x1_%d" % i, writes=[("x1", i)])
        ft = bt
        oP = P23_END
        NPS = 3
        zcP = [[carve(oP + (c * NPS + s_) * 1024, [128, 256]) for s_ in range(NPS)] for c in range(2)]; oP += 2 * NPS * 1024
        ocP = [[carve(oP + (c * NPS + s_) * 1024, [128, 256]) for s_ in range(NPS)] for c in range(2)]; oP += 2 * NPS * 1024
        ofP = [carve(oP + s_ * 1024, [128, 256]) for s_ in range(NPS)]; oP += NPS * 1024
        olnP = [carve(oP + s_ * 1024, [128, 256]) for s_ in range(NPS)]; oP += NPS * 1024
        osqP = [carve(oP + s_ * 512, [128, 256], BF16) for s_ in range(NPS)]; oP += NPS * 512
        assert oP <= P23_END + 32768
        for s_ in range(NPS):
            for kk in [("zcP", 0, s_), ("zcP", 1, s_), ("ocP", 0, s_), ("ocP", 1, s_), ("ofP", s_), ("olnP", s_), ("osqP", s_)]:
                S.seed(kk, ft)
        pairs = [(0, 1), (2, 3)]
        acount = [0]
        gcount = [0]

        pgi = [0]

        def attn_group(hh, q0, nq, ktiles, wkeys):
            units = []
            nkt = len(ktiles)
            ozo = 0
            if nq == 256:
                ozo = 0
                pgi[0] += 1
            if nq == 256:
                bO, bZ = (4, 5) if pgi[0] % 2 == 1 else (6, 7)
                OZK = [PB(bO), PB(bZ)]
                def OB(b_):
                    return {4: bank(bO, 256, 0), 5: bank(bO, 256, 256), 6: bank(bZ, 256, 0), 7: bank(bZ, 256, 256)}[b_]
                def OK_(b_):
                    return [PB(bO)] if b_ in (4, 5) else [PB(bZ)]
            else:
                bO = bZ = None
                OZK = [PB(4), PB(5), PB(6), PB(7)]
                def OB(b_):
                    return bank(b_, nq)
                def OK_(b_):
                    return [PB(b_)]
            for ti, (k_ap, v_ap, kkeys) in enumerate(ktiles):
                u = acount[0]
                acount[0] += 1
                pr = pairs[u % 2]
                er = u % 3
                E = E_r[er]
                first, last = ti == 0, ti == nkt - 1
                qb = q0 // 512

                def sA(k_ap=k_ap, pr=pr, kkeys=kkeys):
                    def mm(h):
                        h.matmul(bank(pr[0], nq), k_ap[0:64, :], qT[0:64, hh, q0:q0 + nq], start=True, stop=True)
                        return h.matmul(bank(pr[1], nq), k_ap[64:128, :], qT[64:128, hh, q0:q0 + nq], start=True, stop=True)
                    S.op("pe", mm, reads=[("qT", hh, qb)] + [kk for kk in kkeys if kk[0] in ("kT", "kTc")], writes=[PB(pr[0]), PB(pr[1])])

                def sB(pr=pr, E=E, er=er):
                    Eo = E[:, :, 0:nq] if nq == 512 else E.rearrange("p c n -> p (c n)")[:, 0:512].rearrange("p (c n) -> p c n", c=2)
                    S.op("act", lambda h: h.activation(
                        out=Eo,
                        in_=ps_all[:, pr[0] * 512: pr[0] * 512 + 1024].rearrange("p (c n) -> p c n", c=2)[:, :, 0:nq],
                        func=AF.Exp, scale=0.125), reads=[PB(pr[0]), PB(pr[1])], writes=[("E", er)])

                def sC(v_ap=v_ap, E=E, er=er, first=first, last=last, kkeys=kkeys, u=u):
                    def mm(h):
                        if nq == 256:
                            Ef = E.rearrange("p c n -> p (c n)")[:, 0:512]
                            h.matmul(bank(bO), v_ap, Ef, start=first, stop=last)
                            return h.matmul(bank(bZ), ones_bf, Ef, start=first, stop=last)
                        h.matmul(OB(4), v_ap, E[:, 0, 0:nq], start=first, stop=last)
                        h.matmul(OB(5), v_ap, E[:, 1, 0:nq], start=first, stop=last)
                        h.matmul(OB(6), ones_bf, E[:, 0, 0:nq], start=first, stop=last)
                        return h.matmul(OB(7), ones_bf, E[:, 1, 0:nq], start=first, stop=last)
                    S.op("pe", mm, reads=[("E", er), "cbf"] + [kk for kk in kkeys if kk[0] in ("V", "Vc")],
                         writes=OZK)
                    if last:
                        g = gcount[0]
                        gcount[0] += 1
                        slot = g % NPS if nq == 256 else 0
                        fc, fc1, fc2 = fin_ab(slot)
                        if nq == 256:
                            e2 = [4, fc2, None]
                            deferred.append([3, fc1, e2])
                            deferred.append(e2)
                        else:
                            deferred.append([5, fc, None])
                    for it in list(deferred):
                        if it not in deferred:
                            continue
                        if it[0] <= 0:
                            deferred.remove(it)
                            it[1](u, None if nq == 512 else (6 if bO == 4 else 4))
                            if nq == 512 and it[2] is not None and it[2] in deferred:
                                deferred.remove(it[2])
                                it[2][1](u, None)
                        else:
                            it[0] -= 1

                def fin_ab(slot=0):
                    if nq == 256:
                        Z0, Z1_, O0, O1_, OF, OSQ, OLN = zcP[0][slot], zcP[1][slot], ocP[0][slot], ocP[1][slot], ofP[slot], osqP[slot], olnP[slot]
                        kz0, kz1, ko0, ko1, kof, kosq, koln = [("zcP", 0, slot)], [("zcP", 1, slot)], [("ocP", 0, slot)], [("ocP", 1, slot)], \
                            [("ofP", slot)], [("osqP", slot)], [("olnP", slot)]
                    else:
                        Z0, Z1_, O0, O1_, OF, OSQ, OLN = zc[0], zc[1], oc[0], oc[1], of, osq, oln
                        kz0, kz1, ko0, ko1, kof, kosq, koln = [("zc", 0)], [("zc", 1)], [("oc", 0)], [("oc", 1)], ["of"], ["osq"], ["oln"]
                    if nq == 256:
                        S.op("act", lambda h: h.activation(out=Z0, in_=OB(6), func=AF.Ln), reads=OK_(6), writes=kz0)
                        S.op("act", lambda h: h.activation(out=Z1_, in_=OB(7), func=AF.Ln), reads=OK_(7), writes=kz1)
                        S.op("act", lambda h: h.activation(out=Z0, in_=Z0, func=AF.Exp, scale=-1.0), reads=kz0, writes=kz0)
                        S.op("act", lambda h: h.activation(out=Z1_, in_=Z1_, func=AF.Exp, scale=-1.0), reads=kz1, writes=kz1)
                        S.op("dve", lambda h: h.tensor_tensor(out=O0, in0=OB(4), in1=Z0, op=ALU.mult), reads=OK_(4) + kz0, writes=ko0)
                        S.op("dve", lambda h: h.tensor_tensor(out=O1_, in0=OB(5), in1=Z1_, op=ALU.mult), reads=OK_(5) + kz1, writes=ko1)
                    else:
                        S.op("dve", lambda h: h.tensor_copy(out=Z0, in_=OB(6)), reads=OK_(6), writes=kz0)
                        S.op("act", lambda h: h.activation(out=Z1_, in_=OB(7), func=AF.Ln), reads=OK_(7), writes=kz1)
                        S.op("act", lambda h: h.activation(out=O0, in_=OB(4), func=AF.Copy), reads=OK_(4), writes=ko0)
                        S.op("dve", lambda h: h.tensor_copy(out=O1_, in_=OB(5)), reads=OK_(5), writes=ko1)
                        S.op("dve", lambda h: h.reciprocal(out=Z0, in_=Z0), reads=kz0, writes=kz0)
                        S.op("act", lambda h: h.activation(out=Z1_, in_=Z1_, func=AF.Exp, scale=-1.0), reads=kz1, writes=kz1)
                        S.op("dve", lambda h: h.tensor_tensor(out=O0, in0=O0, in1=Z0, op=ALU.mult), reads=ko0 + kz0, writes=ko0)
                        S.op("dve", lambda h: h.tensor_tensor(out=O1_, in0=O1_, in1=Z1_, op=ALU.mult), reads=ko1 + kz1, writes=ko1)
                    S.op("dve", lambda h: h.scalar_tensor_tensor(out=OF, in0=O1_, scalar=neglam, in1=O0, op0=ALU.mult, op1=ALU.add),
                         reads=ko0 + ko1 + ["neglam"], writes=kof)

                    st_ = {}

                    def fin_c1(ucur, free_bank=None):
                        mpr = pairs[(ucur + 1) % 2] if free_bank is None else (free_bank, free_bank)
                        st_["mpr"] = mpr
                        S.op("act", lambda h: h.activation(out=OSQ, in_=OF, func=AF.Square), reads=kof, writes=kosq)
                        S.op("pe", lambda h: h.matmul(bank(mpr[0], nq), mean128, OSQ, start=True, stop=True),
                             reads=kosq + ["cbf"], writes=[PB(mpr[0]), PB(mpr[1])])

                    def fin_c2(ucur, free_bank=None):
                        mpr = st_["mpr"]
                        S.op("act", lambda h: h.activation(out=OLN, in_=bank(mpr[0], nq), func=AF.Ln, bias=epsc[:]),
                             reads=[PB(mpr[0]), "epsc"], writes=koln)
                        S.op("act", lambda h: h.activation(out=OLN, in_=OLN, func=AF.Exp, scale=-0.5), reads=koln, writes=koln)
                        S.op("dve", lambda h: h.scalar_tensor_tensor(out=mixT[:, 2 + hh, q0:q0 + nq], in0=OF, scalar=sgs, in1=OLN,
                                                                     op0=ALU.mult, op1=ALU.mult),
                             reads=kof + koln + ["sgs"], writes=wkeys)

                    def fin_c(ucur, free_bank=None):
                        fin_c1(ucur, free_bank)
                        fin_c2(ucur, free_bank)
                    return fin_c, fin_c1, fin_c2

                units.append([sA, sB, sC])
            return units

        all_units = []
        deferred = []
        sgroups, pgroups = [], []
        for p_ in range(2):
            for hh in range(H):
                kts = []
                for j in (8 + 2 * p_, 9 + 2 * p_):
                    kts.append((kT_own[:, hh, j * 128:(j + 1) * 128], V_own[:, j, hh * 128:(hh + 1) * 128],
                                [("kT", hh, 2), ("V", j), ("V", j, 1)]))
                pgroups.append((hh, 1024 + 256 * p_, 256, kts, [("hT", 2 + hh, 2), ("mixp", 2 + hh, p_)]))
        for qb in range(2):
            for hh in range(H):
                kts = []
                for j in range(8):
                    kts.append((kT_own[:, hh, j * 128:(j + 1) * 128], V_own[:, j, hh * 128:(hh + 1) * 128],
                                [("kT", hh, j // 4), ("V", j), ("V", j, 1)]))
                for j in range(4):
                    kts.append((kT_cache[:, hh, j * 128:(j + 1) * 128], V_cache[:, j, hh * 128:(hh + 1) * 128],
                                [("kTc", hh), ("Vc", 0)]))
                sgroups.append((hh, qb * 512, 512, kts, [("hT", 2 + hh, qb)]))
        for gi_ in range(12):
            all_units += attn_group(*pgroups[gi_])
        for gi_ in range(12):
            all_units += attn_group(*sgroups[gi_])
        pipeline(all_units)

        def flush_attention():
            for it in deferred:
                it[1](acount[0] - 1, None)
            del deferred[:]

        def mix_keys(b):
            ks = [("hT", c, b) for c in range(8)]
            if b == 2:
                ks += [("mixp", c, p_) for c in range(8) for p_ in range(2)]
            return ks

        if debug == "mix":
            flush_attention()
            dbg["mixT"] = dout("dbg_mixT", [128, 8, NTOK], BF16)
            S.dma("sp", [lambda h: h.dma_start(out=dbg["mixT"], in_=mixT)], "dbg", reads=mix_keys(0) + mix_keys(1) + mix_keys(2), is_output=True)
            S.finish()
            return nc

        bt4 = S.all_tokens()
        wg_r = [carve(A2 + 24576 + i * 8192, [128, 8, 512], BF16) for i in range(2)]
        wu_r = [carve(A2 + 40960 + i * 8192, [128, 8, 512], BF16) for i in range(2)]
        wd = carve(A2 + 57344, [128, 8, D], BF16)
        uT = carve(A2 + 73728, [128, 8, NTOK], BF16)
        xn5 = [carve(A2 + 73728 + i * 4096, [128, D]) for i in range(4)]
        gsb = carve(A2 + 122880, [128, NTOK])
        tmp4 = [carve(A2 + 122880 + i * 2048, [128, 512]) for i in range(2)]
        t1f = carve(A2 + 129024, [128, NTOK])
        tmp6 = [carve(A2 + 135168 + i * 2048, [128, 512]) for i in range(2)]
        for kk in [("x1", i) for i in range(6, 12)] + [("tmp4", i) for i in range(2)] + [("xn5", i) for i in range(4)] \
                + [("wu", i) for i in range(2)] + [("wg", i) for i in range(2)] + ["wd"]:
            S.seed(kk, bt4)

        wg_v = w_gate.rearrange("(k p) n -> p k n", p=128)
        wu_v = w_up.rearrange("(k p) n -> p k n", p=128)
        wd_v = w_down.rearrange("(c p) n -> p c n", p=128)

        def load_gu(cg, which):
            ncol = 512 if cg < 5 else 256
            slot = cg % 2
            ring, src, nm = (wg_r, wg_v, "wg") if which == 0 else (wu_r, wu_v, "wu")
            S.dma("pool", [lambda h, k=k: h.dma_start(out=ring[slot][:, k, 0:ncol], in_=src[:, k, cg * 512: cg * 512 + ncol])
                           for k in range(8)], "%s%d" % (nm, slot), writes=[(nm, slot)])

        def load_wd(gi):
            f0 = gi * 8
            nf = min(8, NFC - f0)
            S.dma("pool", [lambda h, fl=fl: h.dma_start(out=wd[:, fl, :], in_=wd_v[:, f0 + fl, :]) for fl in range(nf)], "wd", writes=["wd"])

        load_gu(0, 0)
        load_gu(0, 1)
        load_gu(1, 0)
        load_gu(1, 1)
        load_wd(0)

        def phase4(b):
                cnd = 0 if b < 2 else 1
                for tt in range(4):
                    i = b * 4 + tt
                    if i >= 6:
                        S.dma("sp", [lambda h, i=i: h.dma_start(out=x1[i], in_=xin[i * 128:(i + 1) * 128, :])], "x1_%d" % i, writes=[("x1", i)])
                    for n in range(2):
                        pb = (2 * i + n) % 4
                        r = n
                        def mm(h, i=i, n=n, pb=pb):
                            ins = None
                            for k in range(8):
                                ins = h.matmul(bank(pb), mixT[:, k, i * 128:(i + 1) * 128], wout[:, k, n * 512:(n + 1) * 512],
                                               start=(k == 0), stop=(k == 7))
                            return ins
                        S.op("pe", mm, reads=mix_keys(b) + ["wout"], writes=[PB(pb)])
                        S.op("dve", lambda h, n=n, pb=pb, r=r, cnd=cnd: h.tensor_tensor(out=tmp4[r], in0=bank(pb), in1=gate_t[cnd][:, n * 512:(n + 1) * 512],
                                                                                 op=ALU.mult),
                             reads=[PB(pb), ("gate", cnd, n)], writes=[("tmp4", r)])
                        S.op("dve", lambda h, i=i, n=n, r=r: h.tensor_tensor(out=x1[i][:, n * 512:(n + 1) * 512], in0=x1[i][:, n * 512:(n + 1) * 512],
                                                                           in1=tmp4[r], op=ALU.add),
                             reads=[("x1", i), ("tmp4", r)], writes=[("x1", i)])
                        if pend5:
                            pend5.pop(0)()

        def phase5(b):
                tiles = []
                for tt in range(4):
                    i = b * 4 + tt
                    tiles.append((x1[i], ("x1", i), xn5[tt], [("xn5", tt)], ("xn5", tt)))
                norm_block(b, tiles, G2, "G2", 3, "h2T", [4, 5], hT, defer=pend5)

        pend5 = []
        phase4(0)
        flush_attention()
        phase5(0)
        phase4(1)
        assert not pend5
        phase5(1)
        phase4(2)
        assert not pend5
        seqs = [(0, 1024), (1024, 1280), (1280, 1536)]
        dcount = [0]

        def h2_keys(b):
            return mix_keys(b)

        def ffn_A(f, do_gate=(0, 1, 2), do_up=(0, 1, 2)):
            cg, fo = f // 4, (f % 4) * 128
            slot = cg % 2
            for b in do_gate:
                def mm(h, b=b):
                    ins = None
                    for k in range(8):
                        ins = h.matmul(bank(b), wg_r[slot][:, k, fo:fo + 128], hT[:, k, b * 512:(b + 1) * 512], start=(k == 0), stop=(k == 7))
                    return ins
                S.op("pe", mm, reads=[("wg", slot)] + h2_keys(b), writes=[PB(b)])
            for b in do_up:
                def mm(h, b=b):
                    ins = None
                    for k in range(8):
                        ins = h.matmul(bank(3 + b), wu_r[slot][:, k, fo:fo + 128], hT[:, k, b * 512:(b + 1) * 512], start=(k == 0), stop=(k == 7))
                    return ins
                S.op("pe", mm, reads=[("wu", slot)] + h2_keys(b), writes=[PB(3 + b)])

        def ffn_B(f):
            fl = f % 8
            w0 = vec[:, V_CW + f:V_CW + f + 1]
            w1 = vec[:, V_CW + 22 + f:V_CW + 22 + f + 1]
            w2 = vec[:, V_CW + 44 + f:V_CW + 44 + f + 1]
            cb = vec[:, V_CB + f:V_CB + f + 1]
            for b in range(3):
                S.op("act", lambda h, b=b: h.activation(out=gsb[:, b * 512:(b + 1) * 512], in_=bank(b), func=AF.Copy),
                     reads=[PB(b)], writes=[("gsb", b)])
            S.op("act", lambda h: h.activation(out=t1f, in_=gsb, func=AF.Identity, scale=w1, bias=cb),
                 reads=[("gsb", b) for b in range(3)] + ["vec"], writes=["t1f"])
            for (a, e) in seqs:
                S.op("dve", lambda h, a=a, e=e: h.scalar_tensor_tensor(out=t1f[:, a + 1:e], in0=gsb[:, a:e - 1], scalar=w0, in1=t1f[:, a + 1:e],
                                                                   op0=ALU.mult, op1=ALU.add),
                     reads=[("gsb", b) for b in range(3)] + ["t1f", "vec"], writes=["t1f"])
                S.op("dve", lambda h, a=a, e=e: h.scalar_tensor_tensor(out=t1f[:, a:e - 1], in0=gsb[:, a + 1:e], scalar=w2, in1=t1f[:, a:e - 1],
                                                                   op0=ALU.mult, op1=ALU.add),
                     reads=[("gsb", b) for b in range(3)] + ["t1f", "vec"], writes=["t1f"])
            S.op("act", lambda h: h.activation(out=t1f, in_=t1f, func=AF.Silu), reads=["t1f"], writes=["t1f"])

        def ffn_B2(f):
            fl = f % 8
            for b in range(3):
                S.op("dve", lambda h, b=b: h.tensor_tensor(out=uT[:, fl, b * 512:(b + 1) * 512], in0=t1f[:, b * 512:(b + 1) * 512], in1=bank(3 + b),
                                                          op=ALU.mult),
                     reads=["t1f", PB(3 + b)], writes=[("uT", fl, b)])

        def ffn_down_pre(gi, npre):
            f0 = gi * 8
            nf = min(8, NFC - f0)
            pre_banks = [6, 7, 0, 1, 2]
            for o_ in range(npre):
                i, n = o_ // 2, o_ % 2
                b = i // 4
                pb = pre_banks[o_]
                def mm(h, i=i, n=n, pb=pb):
                    ins = None
                    for fl in range(nf - 1):
                        ins = h.matmul(bank(pb), uT[:, fl, i * 128:(i + 1) * 128], wd[:, fl, n * 512:(n + 1) * 512],
                                       start=(fl == 0), stop=False)
                    return ins
                S.op("pe", mm, reads=[("uT", fl, b) for fl in range(nf - 1)] + ["wd"], writes=[PB(pb)])
                pre_done[(i, n)] = pb

        pre_done = {}

        def ffn_down(gi):
            f0 = gi * 8
            nf = min(8, NFC - f0)
            lastg = gi == 2
            for i in range(12):
                b = i // 4
                cnd = 0 if b < 2 else 1
                for n in range(2):
                    r = dcount[0] % 2
                    if lastg and (i, n) in pre_done:
                        pb = pre_done[(i, n)]
                        fls = [nf - 1]
                    else:
                        pb = 6 + dcount[0] % 2
                        fls = list(range(nf))
                    dcount[0] += 1
                    def mm(h, i=i, n=n, pb=pb, fls=fls):
                        ins = None
                        for fl in fls:
                            ins = h.matmul(bank(pb), uT[:, fl, i * 128:(i + 1) * 128], wd[:, fl, n * 512:(n + 1) * 512],
                                           start=(fl == 0), stop=(fl == nf - 1))
                        return ins
                    S.op("pe", mm, reads=[("uT", fl, b) for fl in fls] + ["wd"], writes=[PB(pb)])
                    S.op("dve", lambda h, n=n, pb=pb, r=r, cnd=cnd: h.tensor_tensor(out=tmp6[r], in0=bank(pb), in1=gate_t[cnd][:, n * 512:(n + 1) * 512],
                                                                             op=ALU.mult),
                         reads=[PB(pb), ("gate", cnd, n)], writes=[("tmp6", r)])
                    S.op("dve", lambda h, i=i, n=n, r=r: h.tensor_tensor(out=x1[i][:, n * 512:(n + 1) * 512], in0=x1[i][:, n * 512:(n + 1) * 512],
                                                                       in1=tmp6[r], op=ALU.add),
                         reads=[("x1", i), ("tmp6", r)], writes=[("x1", i)])
                if lastg:
                    S.dma("sp", [lambda h, i=i: h.dma_start(out=yout[i * 128:(i + 1) * 128, :], in_=x1[i])], "y%d" % i,
                          reads=[("x1", i)], is_output=True)

        phase5(2)
        ffn_A(0, do_gate=(0,), do_up=())
        for _ in range(4):
            pend5.pop(0)()
        ffn_A(0, do_gate=(1,), do_up=())
        while pend5:
            pend5.pop(0)()
        wtoks = S.retire(["wout"])
        S.seed(("wa", 0), wtoks)
        S.seed(("wa", 1), wtoks)
        ada_bc(5, [6, 7])
        bt6 = S.all_tokens()
        for kk in [("uT", fl, b) for fl in range(8) for b in range(3)] + [("gsb", b) for b in range(3)] + ["t1f"] + [("tmp6", i) for i in range(2)]:
            S.seed(kk, bt6)
        for f in range(NFC):
            cg = f // 4
            if f == 0:
                ffn_A(0, do_gate=(2,), do_up=(0, 1, 2))
            else:
                ffn_A(f)
            ffn_B(f)
            if f % 8 == 0 and f > 0:
                gi = f // 8 - 1
                ffn_down(gi)
                load_wd(gi + 1)
            if f == NFC - 1:
                ffn_down_pre(2, 5)
            ffn_B2(f)
            if f % 4 == 3 and cg + 2 <= 5:
                load_gu(cg + 2, 0)
                load_gu(cg + 2, 1)
        ffn_down(2)

        S.finish()
    return nc


def _consts():
    bf = ml_dtypes.bfloat16
    ident = np.eye(128, dtype=np.float32)
    blk = np.zeros((128, 128), np.float32)
    blk[:64, :64] = 1.0 / 64
    blk[64:, 64:] = 1.0 / 64
    ones = np.ones((128, 128), np.float32)
    mean = np.full((128, 128), 1.0 / 128, np.float32)
    p = np.arange(128)
    d = p % 64
    j = d % 32
    partner = p - j + (j + 16) % 32
    perm = np.zeros((128, 128), np.float32)
    perm[partner, p] = 1.0
    cbf = np.stack([blk, ones, mean, perm], 1).astype(bf)
    t = np.arange(1024)
    row = (t // 64).astype(np.float64)
    col = (t % 64).astype(np.float64)
    inv = 10000.0 ** (-np.arange(16) / 16.0)
    half = d // 32
    f = j % 16
    mem = j // 16
    pos = np.where(half[:, None] == 0, row[None, :], col[None, :])
    ang = pos * inv[f][:, None]
    cosT = np.cos(ang)
    sinT = np.sin(ang) * np.where(mem == 0, -1.0, 1.0)[:, None]
    rope = np.stack([cosT, sinT], 1).astype(np.float32)
    dd = np.arange(64)
    a = 2 * np.pi * np.outer(dd, dd) / 64
    C = np.zeros((128, 128)); Sn = np.zeros((128, 128))
    for hh in range(2):
        C[hh * 64:(hh + 1) * 64, hh * 64:(hh + 1) * 64] = np.cos(a)
        Sn[hh * 64:(hh + 1) * 64, hh * 64:(hh + 1) * 64] = np.sin(a)
    dftc = np.concatenate([C, Sn], 1).astype(bf)

    def posdft(n):
        tt = np.arange(n)
        aa = 2 * np.pi * ((np.outer(tt, tt)) % n) / n
        sc = 1.0 / np.sqrt(n * 64.0)
        m = np.concatenate([np.cos(aa) * sc, -np.sin(aa) * sc], 1)
        return m.reshape(n // 128, 128, 2 * n).transpose(1, 0, 2).astype(bf)

    def posdft_half(n):
        tt = np.arange(n)
        aa = 2 * np.pi * ((np.outer(tt, tt[:n // 2])) % n) / n
        sc = 1.0 / np.sqrt(n * 64.0)
        m = np.concatenate([np.cos(aa) * sc, -np.sin(aa) * sc], 1)
        return m.reshape(n // 128, 128, n).transpose(1, 0, 2).astype(bf)

    return dict(c_ident=ident, c_bf=cbf, c_rope=rope, c_dftc=dftc, c_dfts=posdft_half(1024), c_dftp=posdft(256))


_CACHE = {}


def _in_maps(inp):
    f = lambda a: np.ascontiguousarray(np.asarray(a, dtype=np.float32))
    cs = _consts()
    shared = dict(cs)
    for k in ["w_ada", "w_in", "w_out", "w_gate", "w_up", "w_down"]:
        shared[k] = f(inp[k][0])
    shared["b_ada"] = f(inp["b_ada"]).reshape(1, 6 * D)
    shared["lamv"] = f(np.concatenate([inp["lam_q1"][0], inp["lam_k1"][0], inp["lam_q2"][0], inp["lam_k2"][0]])).reshape(1, 256)
    shared["qkg"] = f(np.concatenate([inp["q_norm_g"][0], inp["k_norm_g"][0]])).reshape(1, 128)

    def col(v, n):
        return np.asarray(v, np.float32).reshape(n, 128).T

    maps = []
    for i in range(NCORES):
        vt = np.zeros((128, NV), np.float32)
        vt[:, V_BADA:V_BADA + 48] = col(inp["b_ada"][0], 48)
        vt[:, V_N1:V_N1 + 8] = col(inp["norm1_g"][0], 8)
        vt[:, V_N2:V_N2 + 8] = col(inp["norm2_g"][0], 8)
        vt[:, V_CS:V_CS + 8] = col(inp["c"][i], 8)
        vt[:, V_CC:V_CC + 8] = col(inp["c_ctx"], 8)
        vt[:, V_CW:V_CW + 66] = col(np.asarray(inp["conv_w"][0]).reshape(-1), 66)
        vt[:, V_CB:V_CB + 22] = col(inp["conv_b"][0], 22)
        vt[:, V_QG] = np.tile(np.asarray(inp["q_norm_g"][0], np.float32), 2)
        vt[:, V_KG] = np.tile(np.asarray(inp["k_norm_g"][0], np.float32), 2)
        vt[:, V_SG] = np.asarray(inp["subln_g"][0], np.float32)
        vt[:, V_SIGN] = np.where(np.arange(128) % 2 == 0, 1.0, -1.0)
        m = dict(shared)
        m["vecT"] = vt
        m["xin"] = f(np.concatenate([inp["x_sample"][i], inp["x_prompt"][2 * i], inp["x_prompt"][2 * i + 1]], 0))
        m["ck"] = f(np.asarray(inp["cache_k"][i, 0]).reshape(H, 512, 128))
        m["cv"] = f(inp["cache_v"][i, 0])
        maps.append(m)
    return maps


def kernel(**inp):
    if "nc" not in _CACHE:
        _CACHE["nc"] = build_nc(DEBUG)
    nc = _CACHE["nc"]
    maps = _in_maps(inp)
    res = run_bass_kernel_spmd(nc, maps, core_ids=list(range(NCORES)))
    if DEBUG:
        return res
    r = res.results
    ys = np.stack([r[i]["yout"][:1024] for i in range(NCORES)], 0)
    yp = np.concatenate([r[i]["yout"][1024:].reshape(2, 256, D) for i in range(NCORES)], 0)
    nk = np.concatenate([r[i]["nk"] for i in range(NCORES)], 0).reshape(16, 1, H, 256, 2, 64)
    nv = np.concatenate([r[i]["nv"] for i in range(NCORES)], 0).reshape(16, 1, H, 256, 128)
    return (yp.astype(np.float32), ys.astype(np.float32), nk.astype(np.float32), nv.astype(np.float32))
```
